# Optimizing a Trainium2 kernel written in Bass

```python
import jax
import jax.numpy as jnp
from jax import lax
import numpy as np

D_MODEL = 1024
BATCH = 2
SEQ = 8192
DEPTH = 1
DEC_BATCH = 128
DEC_SEQ = 1
PAST_LEN = 8192
PAGE_SIZE = 128

HEAD_DIM = 64
HEADS_PER_GROUP = 4
DILATED_GROUPS = ((128, 1), (512, 4), (2048, 16))
N_A_HEADS = HEADS_PER_GROUP * len(DILATED_GROUPS)
A_WIDTH = N_A_HEADS * HEAD_DIM
A_OUT = HEADS_PER_GROUP * HEAD_DIM
ROT_DIM = HEAD_DIM // 4
ROPE_THETA = 500000.0
Q_BLOCK = 128

GLA_HEADS = 4
GLA_DK = D_MODEL // 2 // GLA_HEADS
GLA_DV = D_MODEL // GLA_HEADS
GLA_K = GLA_HEADS * GLA_DK
GLA_V = GLA_HEADS * GLA_DV
GATE_RANK = 16
GATE_NORMALIZER = 16.0
GLA_CHUNK = 64

D_FF = -(-8 * D_MODEL // (3 * 256)) * 256
PLE_DIM = 256
DN_ALPHA = (2 * DEPTH) ** 0.25
DN_BETA = (8 * DEPTH) ** -0.25
LN_EPS = 1e-5
RMS_EPS = 1e-6
IN_COLS = 3 * A_WIDTH + 2 * GLA_K + 2 * GLA_V + GATE_RANK + 2 * D_MODEL

kernel_name = 'dilated_swa_gla_hybrid_step'


def _split_points():
    sizes = (A_WIDTH, A_WIDTH, A_WIDTH, GLA_K, GLA_K, GLA_V, GLA_V, GATE_RANK, D_MODEL, D_MODEL)
    return [int(c) for c in np.cumsum(sizes)[:-1]]


def _layernorm(x, g, b):
    xf = x.astype(jnp.float32)
    xc = xf - jnp.mean(xf, -1, keepdims=True)
    var = jnp.mean(xc * xc, -1, keepdims=True)
    y = xc * lax.rsqrt(var + LN_EPS) * g.astype(jnp.float32) + b.astype(jnp.float32)
    return y.astype(x.dtype)


def _rope_partial(x, pos):
    half = ROT_DIM // 2
    inv_freq = ROPE_THETA ** (-jnp.arange(0, ROT_DIM, 2, dtype=jnp.float32) / ROT_DIM)
    ang = pos.astype(jnp.float32)[:, None] * inv_freq[None, :]
    cos = jnp.cos(ang)[:, None, :]
    sin = jnp.sin(ang)[:, None, :]
    xr = x[..., :ROT_DIM].astype(jnp.float32)
    x1, x2 = xr[..., :half], xr[..., half:]
    rot = jnp.concatenate([x1 * cos - x2 * sin, x2 * cos + x1 * sin], -1)
    return jnp.concatenate([rot.astype(x.dtype), x[..., ROT_DIM:]], -1)


def _dilated_prompt(q, k, v, window, dil):
    b, s, h, e = q.shape
    nj = window // dil
    l = s // dil
    nb = -(-l // Q_BLOCK)
    lp = nb * Q_BLOCK

    def to_sub(t):
        t = t.reshape(b, l, dil, h, e).transpose(0, 2, 3, 1, 4)
        return jnp.pad(t, ((0, 0), (0, 0), (0, 0), (0, lp - l), (0, 0)))

    def band(t):
        t = jnp.pad(t, ((0, 0), (0, 0), (0, 0), (Q_BLOCK, 0), (0, 0)))
        t = t.reshape(b, dil, h, nb + 1, Q_BLOCK, e)
        return jnp.concatenate([t[:, :, :, :-1], t[:, :, :, 1:]], axis=4)

    qb = to_sub(q).reshape(b, dil, h, nb, Q_BLOCK, e).astype(jnp.float32)
    kb = band(to_sub(k)).astype(jnp.float32)
    vb = band(to_sub(v)).astype(jnp.float32)
    qi = np.arange(Q_BLOCK)[:, None]
    kc = np.arange(2 * Q_BLOCK)[None, :]
    back = qi + Q_BLOCK - kc
    key_idx = np.arange(nb)[:, None, None] * Q_BLOCK + kc[None] - Q_BLOCK
    mask = (back >= 0) & (back <= nj) & (key_idx >= 0)
    scores = jnp.einsum('bghnqe,bghnke->bghnqk', qb, kb) * (e ** -0.5)
    scores = jnp.where(mask, scores, -jnp.inf)
    lse = jax.nn.logsumexp(scores, axis=-1)
    probs = jnp.exp(scores - lse[..., None])
    o = jnp.einsum('bghnqk,bghnke->bghnqe', probs, vb)
    o = o.reshape(b, dil, h, lp, e)[:, :, :, :l].transpose(0, 3, 1, 2, 4).reshape(b, s, h, e)
    lse = lse.reshape(b, dil, h, lp)[..., :l].transpose(0, 3, 1, 2).reshape(b, s, h)
    return o, lse


def _dilated_sample(q, k_new, v_new, buf, window, dil):
    n, t, h, e = q.shape
    lb = buf.shape[1]
    nj = window // dil
    kf = jnp.concatenate([buf[:, :, 0], k_new.astype(buf.dtype)], axis=1)
    vf = jnp.concatenate([buf[:, :, 1], v_new.astype(buf.dtype)], axis=1)
    idx = lb + np.arange(t)[:, None] - dil * np.arange(nj + 1)[None, :]
    valid = idx >= 0
    idx = np.maximum(idx, 0)
    kg = kf[:, idx].astype(jnp.float32)
    vg = vf[:, idx].astype(jnp.float32)
    scores = jnp.einsum('nthe,ntjhe->nthj', q.astype(jnp.float32), kg) * (e ** -0.5)
    scores = jnp.where(valid[None, :, None, :], scores, -jnp.inf)
    lse = jax.nn.logsumexp(scores, axis=-1)
    probs = jnp.exp(scores - lse[..., None])
    o = jnp.einsum('nthj,ntjhe->nthe', probs, vg)
    keep = min(window, lb + t)
    new_buf = jnp.stack([kf[:, lb + t - keep:], vf[:, lb + t - keep:]], axis=2)
    return o, lse, new_buf


def _combine_groups(outs, lses):
    o = jnp.stack(outs, 0)
    w = jax.nn.softmax(jnp.stack(lses, 0), axis=0)
    o = jnp.sum(w[..., None] * o, axis=0)
    return o.reshape(o.shape[0], o.shape[1], A_OUT)


def _gla_prompt(q, k, v, loga):
    b, s, h, _ = q.shape
    nc = s // GLA_CHUNK
    causal = np.tril(np.ones((GLA_CHUNK, GLA_CHUNK), dtype=bool))[:, :, None]

    def chunks(t):
        return t.reshape(b, nc, GLA_CHUNK, h, t.shape[-1]).transpose(1, 0, 3, 2, 4)

    def step(st, inp):
        qc, kc, vc, gc = inp
        cum = jnp.cumsum(gc, axis=2)
        o_inter = jnp.einsum('bhcd,bhdv->bhcv', qc * jnp.exp(cum), st)
        diff = cum[:, :, :, None, :] - cum[:, :, None, :, :]
        decay = jnp.exp(jnp.where(causal, diff, -jnp.inf))
        att = jnp.einsum('bhid,bhjd,bhijd->bhij', qc, kc, decay)
        o_intra = jnp.einsum('bhij,bhjv->bhiv', att, vc)
        last = cum[:, :, -1:, :]
        k_dec = kc * jnp.exp(last - cum)
        st = jnp.exp(last[:, :, 0, :])[..., None] * st + jnp.einsum('bhjd,bhjv->bhdv', k_dec, vc)
        return st, o_inter + o_intra

    st0 = jnp.zeros((b, h, GLA_DK, GLA_DV), jnp.float32)
    st, o = lax.scan(step, st0, (chunks(q), chunks(k), chunks(v), chunks(loga)))
    o = o.transpose(1, 0, 3, 2, 4).reshape(b, s, h, GLA_DV)
    return o, st


def _gla_sample(q, k, v, loga, st0):
    def step(st, inp):
        qt, kt, vt, gt = inp
        st = jnp.exp(gt)[..., None] * st + kt[..., :, None] * vt[..., None, :]
        return st, jnp.einsum('nhd,nhdv->nhv', qt, st)

    seq = tuple(t.transpose(1, 0, 2, 3) for t in (q, k, v, loga))
    st, o = lax.scan(step, st0.astype(jnp.float32), seq)
    return o.transpose(1, 0, 2, 3), st


def _head_rmsnorm(o, g):
    o = o * lax.rsqrt(jnp.mean(o * o, -1, keepdims=True) + RMS_EPS)
    o = o * g.astype(jnp.float32).reshape(GLA_HEADS, GLA_DV)
    return o.reshape(o.shape[0], o.shape[1], GLA_V)


def _layer(x, pe, pos, bufs, gla_state, w_in, w_gate_up, b_gate, gla_norm_g, w_a_out, w_b_out, w_o,
           ln1_g, ln1_b, w_ff_gate, w_ff_up, w_ff_down, ln2_g, ln2_b, w_ple_gate, w_ple_proj):
    nb, t, _ = x.shape
    qa, ka, va, qg, kg, vg, rg, glr, ga, gb = jnp.split(x @ w_in, _split_points(), axis=-1)
    qa = _rope_partial(qa.reshape(nb, t, N_A_HEADS, HEAD_DIM), pos)
    ka = _rope_partial(ka.reshape(nb, t, N_A_HEADS, HEAD_DIM), pos)
    va = va.reshape(nb, t, N_A_HEADS, HEAD_DIM)
    outs, lses, new_bufs = [], [], []
    for g, (window, dil) in enumerate(DILATED_GROUPS):
        sl = slice(g * HEADS_PER_GROUP, (g + 1) * HEADS_PER_GROUP)
        qh, kh, vh = qa[:, :, sl], ka[:, :, sl], va[:, :, sl]
        if bufs is None:
            o, lse = _dilated_prompt(qh, kh, vh, window, dil)
            keep = min(window, t)
            nbuf = jnp.stack([kh[:, t - keep:], vh[:, t - keep:]], axis=2)
        else:
            o, lse, nbuf = _dilated_sample(qh, kh, vh, bufs[g], window, dil)
        outs.append(o)
        lses.append(lse)
        new_bufs.append(nbuf)
    o_a = _combine_groups(outs, lses).astype(x.dtype)
    loga = jax.nn.log_sigmoid((glr @ w_gate_up + b_gate).astype(jnp.float32)) / GATE_NORMALIZER
    loga = loga.reshape(nb, t, GLA_HEADS, GLA_DK)
    qg = qg.astype(jnp.float32).reshape(nb, t, GLA_HEADS, GLA_DK) * (GLA_DK ** -0.5)
    kg = kg.astype(jnp.float32).reshape(nb, t, GLA_HEADS, GLA_DK)
    vg = vg.astype(jnp.float32).reshape(nb, t, GLA_HEADS, GLA_DV)
    if gla_state is None:
        o_b, st = _gla_prompt(qg, kg, vg, loga)
        st = st.astype(x.dtype)
    else:
        o_b, st = _gla_sample(qg, kg, vg, loga, gla_state)
        st = st.astype(gla_state.dtype)
    o_b = (_head_rmsnorm(o_b, gla_norm_g) * jax.nn.silu(rg.astype(jnp.float32))).astype(x.dtype)
    merged = jax.nn.sigmoid(ga) * (o_a @ w_a_out) + jax.nn.sigmoid(gb) * (o_b @ w_b_out)
    x = _layernorm(DN_ALPHA * x + merged @ w_o, ln1_g, ln1_b)
    ff = (jax.nn.silu(x @ w_ff_gate) * (x @ w_ff_up)) @ w_ff_down
    x = _layernorm(DN_ALPHA * x + ff, ln2_g, ln2_b)
    x = x + jax.nn.sigmoid(x @ w_ple_gate) * (pe.astype(x.dtype) @ w_ple_proj)
    return x, new_bufs, st


def setup_inputs(seed: int = 0) -> dict:
    key = jax.random.key(seed)
    ks = jax.random.split(key, 24)

    def nrm(k, shape, scale):
        return scale * jax.random.normal(k, shape, jnp.float32)

    lb = [min(w, PAST_LEN) for w, _ in DILATED_GROUPS]
    return {
        'x_prompt': nrm(ks[0], (BATCH, SEQ, D_MODEL), 1.0),
        'x_sample': nrm(ks[1], (DEC_BATCH, DEC_SEQ, D_MODEL), 1.0),
        'cache_a1_kv': nrm(ks[2], (DEPTH, DEC_BATCH, lb[0], 2, HEADS_PER_GROUP, HEAD_DIM), 1.0),
        'cache_a2_kv': nrm(ks[3], (DEPTH, DEC_BATCH, lb[1], 2, HEADS_PER_GROUP, HEAD_DIM), 1.0),
        'cache_a3_kv': nrm(ks[4], (DEPTH, DEC_BATCH, lb[2], 2, HEADS_PER_GROUP, HEAD_DIM), 1.0),
        'state_gla': nrm(ks[5], (DEPTH, DEC_BATCH, GLA_HEADS, GLA_DK, GLA_DV), 1.0),
        'p_prompt': nrm(ks[6], (DEPTH, BATCH, SEQ, PLE_DIM), 1.0),
        'p_sample': nrm(ks[7], (DEPTH, DEC_BATCH, DEC_SEQ, PLE_DIM), 1.0),
        'w_in': nrm(ks[8], (DEPTH, D_MODEL, IN_COLS), D_MODEL ** -0.5),
        'w_gate_up': nrm(ks[9], (DEPTH, GATE_RANK, GLA_K), GATE_RANK ** -0.5),
        'b_gate': nrm(ks[10], (DEPTH, GLA_K), 0.1),
        'gla_norm_g': 1.0 + nrm(ks[11], (DEPTH, GLA_V), 0.02),
        'w_a_out': nrm(ks[12], (DEPTH, A_OUT, D_MODEL), A_OUT ** -0.5),
        'w_b_out': nrm(ks[13], (DEPTH, GLA_V, D_MODEL), GLA_V ** -0.5),
        'w_o': nrm(ks[14], (DEPTH, D_MODEL, D_MODEL), DN_BETA * D_MODEL ** -0.5),
        'ln1_g': 1.0 + nrm(ks[15], (DEPTH, D_MODEL), 0.02),
        'ln1_b': nrm(ks[16], (DEPTH, D_MODEL), 0.02),
        'w_ff_gate': nrm(ks[17], (DEPTH, D_MODEL, D_FF), D_MODEL ** -0.5),
        'w_ff_up': nrm(ks[18], (DEPTH, D_MODEL, D_FF), D_MODEL ** -0.5),
        'w_ff_down': nrm(ks[19], (DEPTH, D_FF, D_MODEL), DN_BETA * D_FF ** -0.5),
        'ln2_g': 1.0 + nrm(ks[20], (DEPTH, D_MODEL), 0.02),
        'ln2_b': nrm(ks[21], (DEPTH, D_MODEL), 0.02),
        'w_ple_gate': nrm(ks[22], (DEPTH, D_MODEL, D_MODEL), D_MODEL ** -0.5),
        'w_ple_proj': nrm(ks[23], (DEPTH, PLE_DIM, D_MODEL), PLE_DIM ** -0.5),
    }


def reference(x_prompt, x_sample, cache_a1_kv, cache_a2_kv, cache_a3_kv, state_gla, p_prompt, p_sample,
              w_in, w_gate_up, b_gate, gla_norm_g, w_a_out, w_b_out, w_o, ln1_g, ln1_b,
              w_ff_gate, w_ff_up, w_ff_down, ln2_g, ln2_b, w_ple_gate, w_ple_proj):
    pos_p = jnp.arange(x_prompt.shape[1])
    pos_s = PAST_LEN + jnp.arange(x_sample.shape[1])
    x_p, x_s = x_prompt, x_sample
    kv_p = [[], [], []]
    kv_s = [[], [], []]
    st_p, st_s = [], []
    for i in range(DEPTH):
        lw = (w_in[i], w_gate_up[i], b_gate[i], gla_norm_g[i], w_a_out[i], w_b_out[i], w_o[i],
              ln1_g[i], ln1_b[i], w_ff_gate[i], w_ff_up[i], w_ff_down[i], ln2_g[i], ln2_b[i],
              w_ple_gate[i], w_ple_proj[i])
        x_p, bufs_p, sp = _layer(x_p, p_prompt[i], pos_p, None, None, *lw)
        x_s, bufs_s, ss = _layer(x_s, p_sample[i], pos_s,
                                 (cache_a1_kv[i], cache_a2_kv[i], cache_a3_kv[i]), state_gla[i], *lw)
        for g in range(len(DILATED_GROUPS)):
            kv_p[g].append(bufs_p[g])
            kv_s[g].append(bufs_s[g])
        st_p.append(sp)
        st_s.append(ss)
    kv_a1_prompt = jnp.stack(kv_p[0])
    kv_a2_prompt = jnp.stack(kv_p[1])
    kv_a3_prompt = jnp.stack(kv_p[2])
    state_gla_prompt = jnp.stack(st_p)
    kv_a1_sample = jnp.stack(kv_s[0])
    kv_a2_sample = jnp.stack(kv_s[1])
    kv_a3_sample = jnp.stack(kv_s[2])
    state_gla_sample = jnp.stack(st_s)
    return (x_p, x_s, kv_a1_prompt, kv_a2_prompt, kv_a3_prompt, state_gla_prompt,
            kv_a1_sample, kv_a2_sample, kv_a3_sample, state_gla_sample)
```

```python
import contextlib
import numpy as np
import concourse.bass as bass
import concourse.mybir as mybir
from concourse.bass_utils import run_bass_kernel_spmd

F32 = mybir.dt.float32
BF16 = mybir.dt.bfloat16
AF = mybir.ActivationFunctionType
ALU = mybir.AluOpType
AX = mybir.AxisListType

NCORES = 8
D = 1024
SEQ = 8192
SEG = 2048
NPRE = 3 * SEG
A_W = 768
DFF = 2816
IN_COLS = 7440
C_QA, C_KA, C_VA, C_QG, C_KG, C_VG, C_RG, C_GLR, C_GA, C_GB = 0, 768, 1536, 2304, 2816, 3328, 4352, 5376, 5392, 6416
DIL = (1, 4, 16)
ALPHA = float(2.0 ** 0.25)
LN_EPS = 1e-5
RMS_EPS = 1e-6
NS = 16
ENGS = ("pe", "act", "dve", "pool", "sp")


class Buf:
    __slots__ = ("name", "lw", "rd", "dsem", "dcnt")

    def __init__(self, name):
        self.name = name
        self.lw = None
        self.rd = []
        self.dsem = None
        self.dcnt = 0


class Ins:
    __slots__ = ("eng", "fn", "deps", "needed", "ticket", "is_dma", "dsem", "dval", "idx", "tag")


class Sched:
    def __init__(self, nc, stack):
        self.nc = nc
        self.stack = stack
        self.q = {e: [] for e in ENGS}
        self.esem = {e: stack.enter_context(nc.semaphore("es_" + e)) for e in ENGS}
        self.nsem = 0
        self.bar = {e: None for e in ENGS}
        self.alldma = []

    def newsem(self):
        self.nsem += 1
        return self.stack.enter_context(self.nc.semaphore("ds%d" % self.nsem))

    def op(self, eng, fn, r=(), w=(), dma=None):
        ins = Ins()
        ins.eng = eng
        ins.fn = fn
        ins.needed = False
        ins.ticket = 0
        ins.is_dma = dma is not None
        ins.idx = len(self.q[eng])
        ins.tag = 0
        if TAGS is not None:
            import sys as _sys
            fr = _sys._getframe(1)
            while fr.f_code.co_name in ("mm", "evac", "op"):
                fr = fr.f_back
            ins.tag = fr.f_lineno
        deps = set()
        for b in r:
            if b.lw is not None:
                deps.add(b.lw)
        for b in w:
            if b.lw is not None:
                deps.add(b.lw)
            deps.update(b.rd)
        if self.bar[eng] is not None:
            deps.update(self.bar[eng])
            self.bar[eng] = None
        deps.discard(ins)
        best = {}
        for d in deps:
            if d.is_dma:
                key = ("d", id(d.dsem))
                val = d.dval
            else:
                if d.eng == "pe" and eng == "pe" and not ins.is_dma:
                    continue
                key = ("e", d.eng)
                val = d.idx
            if key not in best or best[key][0] < val:
                best[key] = (val, d)
        ins.deps = [v[1] for v in best.values()]
        if ins.is_dma:
            if dma.dsem is None:
                dma.dsem = self.newsem()
            dma.dcnt += 16
            ins.dsem = dma.dsem
            ins.dval = dma.dcnt
            self.alldma.append(ins)
        for b in w:
            b.lw = ins
            b.rd = []
        for b in r:
            b.rd.append(ins)
        self.q[eng].append(ins)
        return ins

    def barrier(self):
        deps = []
        for e in ENGS:
            if self.q[e]:
                deps.append(self.q[e][-1])
        last = {}
        for d in self.alldma:
            last[id(d.dsem)] = d
        deps.extend(last.values())
        for e in ENGS:
            self.bar[e] = list(deps)

    def emit(self):
        nc = self.nc
        for e in ENGS:
            for ins in self.q[e]:
                for d in ins.deps:
                    d.needed = True
        for e in ENGS:
            cnt = 0
            for ins in self.q[e]:
                if ins.needed and not ins.is_dma:
                    cnt += 1
                    ins.ticket = cnt
        esem = self.esem
        if TAGS is not None:
            for e in ENGS:
                TAGS[e] = [(ins.tag, ins.fn is not None, ins.is_dma) for ins in self.q[e]]

        def run(e, eo):
            seen = {}
            for ins in self.q[e]:
                waits = {}
                for d in ins.deps:
                    if d.is_dma:
                        key, sem, val = ("d", id(d.dsem)), d.dsem, d.dval
                    else:
                        key, sem, val = ("e", d.eng), esem[d.eng], d.ticket
                    if seen.get(key, 0) >= val:
                        continue
                    if key not in waits or waits[key][1] < val:
                        waits[key] = (sem, val)
                for key, (sem, val) in waits.items():
                    eo.wait_ge(sem, val)
                    seen[key] = val
                if ins.fn is None:
                    continue
                h = ins.fn(eo)
                if ins.is_dma:
                    h.then_inc(ins.dsem, 16)
                elif ins.needed:
                    h.then_inc(esem[e], 1)

        with nc.Block() as block:
            @block.tensor
            def _(eo):
                run("pe", eo)

            @block.scalar
            def _(eo):
                run("act", eo)

            @block.vector
            def _(eo):
                run("dve", eo)

            @block.gpsimd
            def _(eo):
                run("pool", eo)

            @block.sync
            def _(eo):
                run("sp", eo)


class Arena:
    def __init__(self, nc, stack, name, nwords):
        self.t = stack.enter_context(nc.sbuf_tensor(name, [128, nwords], F32))
        self.n = nwords
        self.off = 0
        self.name = name
        self.k = 0

    def reset(self):
        self.off = 0

    def alloc(self, shape, dtype):
        per = int(np.prod(shape[1:]))
        words = per if dtype == F32 else (per + 1) // 2
        words = (words + 7) // 8 * 8
        assert self.off + words <= self.n, (self.name, self.off, words, self.n)
        ap = self.t[0:shape[0], self.off:self.off + words]
        self.off += words
        if dtype != F32:
            ap = ap.bitcast(dtype)
        ap = ap[:, 0:per]
        if len(shape) == 3:
            ap = ap.rearrange("p (a b) -> p a b", b=shape[2])
        elif len(shape) == 4:
            ap = ap.rearrange("p (a b c) -> p a b c", b=shape[2], c=shape[3])
        self.k += 1
        return ap, Buf("%s_%d" % (self.name, self.k))


def build_program():
    wl = []
    nc = bass.Bass("TRN2", target_bir_lowering=False)
    with contextlib.ExitStack() as stack:
        _build(nc, stack, None, wl)
    del PHASES[:]
    nc = bass.Bass("TRN2", target_bir_lowering=False)
    with contextlib.ExitStack() as stack:
        _build(nc, stack, wl, [])
    return nc


def _build(nc, stack, prelist, wlist):
    def din(name, shape):
        return nc.dram_tensor(name, list(shape), F32, kind="ExternalInput").ap()

    def dout(name, shape):
        return nc.dram_tensor(name, list(shape), F32, kind="ExternalOutput").ap()

    xseg = din("xseg", (NPRE + SEG, D))
    pp = din("pp", (SEG, 256))
    rope = din("rope", (128, 32, 16))
    cst = din("cst", (128, 640))
    w_in = din("w_in", (D, IN_COLS))
    w_gu = din("w_gu", (16, 512))
    vecs = din("vecs", (6, D))
    w_a_out = din("w_a_out", (256, D))
    w_b_out = din("w_b_out", (D, D))
    w_o = din("w_o", (D, D))
    w_fg = din("w_fg", (D, DFF))
    w_fu = din("w_fu", (D, DFF))
    w_fd = din("w_fd", (DFF, D))
    w_pg = din("w_pg", (D, D))
    w_pp = din("w_pp", (256, D))

    xs = din("xs", (128, D))
    pps = din("pps", (128, 256))
    rope_s_d = din("rope_s", (128, 16))
    cst2 = din("cst2", (128, 296))
    c1 = din("c1", (NS, 128, 512))
    c2 = din("c2", (NS, 512, 512))
    c3 = din("c3", (NS, 2048, 512))
    st_in = din("st_in", (NS, 4, 128, 256))
    y_s = dout("y_s", (128, D))
    kvs1 = dout("kvs1", (NS, 128, 512))
    kvs2 = dout("kvs2", (NS, 512, 512))
    kvs3 = dout("kvs3", (NS, 2048, 512))
    st_out = dout("st_out", (NS, 4, 128, 256))
    y_own = dout("y_own", (SEG, D))
    kv_own = dout("kv_own", (SEG, 2, 12, 64))
    st_own = dout("st_own", (4, 128, 256))

    S = Sched(nc, stack)
    op = S.op

    psb = []
    for i in range(8):
        t = stack.enter_context(nc.psum_tensor("ps%d" % i, [128, 512], F32))
        psb.append((t, Buf("ps%d" % i)))
    psi = [0]
    NB = [8]

    def bank():
        i = psi[0]
        psi[0] = (i + 1) % NB[0]
        return psb[i]

    AC = Arena(nc, stack, "consts", 2400)
    ident_f, B_c = AC.alloc([128, 128], F32)
    triu_f, _ = AC.alloc([128, 128], F32)
    ident_b, _ = AC.alloc([128, 128], BF16)
    maskP, _ = AC.alloc([128, 256], BF16)
    ones_b, _ = AC.alloc([128, 64], BF16)
    hval_b, _ = AC.alloc([128, 64], BF16)
    ropet, _ = AC.alloc([128, 32, 16], F32)
    wgu_b, _ = AC.alloc([16, 512], BF16)
    eps_ln, _ = AC.alloc([128, 1], F32)
    eps_rms, _ = AC.alloc([128, 1], F32)
    one_c, _ = AC.alloc([128, 1], F32)
    B_c = Buf("consts")

    op("sp", lambda e: e.dma_start(out=ident_f, in_=cst[:, 0:128]), w=[B_c], dma=B_c)
    op("sp", lambda e: e.dma_start(out=triu_f, in_=cst[:, 128:256]), w=[B_c], dma=B_c)
    op("sp", lambda e: e.dma_start(out=ropet, in_=rope), w=[B_c], dma=B_c)
    op("pool", lambda e: e.dma_start(out=ident_b, in_=cst[:, 0:128]), w=[B_c], dma=B_c)
    op("pool", lambda e: e.dma_start(out=maskP[:, 0:128], in_=cst[:, 256:384]), w=[B_c], dma=B_c)
    op("pool", lambda e: e.dma_start(out=maskP[:, 128:256], in_=cst[:, 128:256]), w=[B_c], dma=B_c)
    op("pool", lambda e: e.dma_start(out=ones_b, in_=cst[:, 384:448]), w=[B_c], dma=B_c)
    op("pool", lambda e: e.dma_start(out=hval_b, in_=cst[:, 512:576]), w=[B_c], dma=B_c)
    op("pool", lambda e: e.dma_start(out=wgu_b, in_=w_gu), w=[B_c], dma=B_c)
    rope_s, _ = AC.alloc([128, 16], F32)
    eye16b, _ = AC.alloc([128, 256], F32)
    bmask, _ = AC.alloc([128, 2, 4], F32)
    lncol, _ = AC.alloc([128, 4, 8], F32)
    bgate_bc, _ = AC.alloc([128, 512], F32)
    op("sp", lambda e: e.dma_start(out=bgate_bc, in_=vecs[5:6, 0:512].broadcast_to([128, 512])), w=[B_c], dma=B_c)
    op("sp", lambda e: e.dma_start(out=rope_s, in_=rope_s_d), w=[B_c], dma=B_c)
    op("sp", lambda e: e.dma_start(out=eye16b, in_=cst2[:, 0:256]), w=[B_c], dma=B_c)
    op("sp", lambda e: e.dma_start(out=bmask, in_=cst2[:, 256:264].rearrange("p (a b) -> p a b", b=4)), w=[B_c], dma=B_c)
    op("sp", lambda e: e.dma_start(out=lncol, in_=cst2[:, 264:296].rearrange("p (a b) -> p a b", b=8)), w=[B_c], dma=B_c)
    op("dve", lambda e: e.memset(eps_ln, LN_EPS), w=[B_c])
    op("dve", lambda e: e.memset(eps_rms, RMS_EPS), w=[B_c])
    op("dve", lambda e: e.memset(one_c, 1.0), w=[B_c])

    NWB = 3
    AW = Arena(nc, stack, "wbuf", NWB * 3072)
    wslots = [AW.alloc([128, 8, 768], BF16) for _ in range(NWB)]
    wi = [0]
    wsrc = {"w_in": w_in, "w_a_out": w_a_out, "w_b_out": w_b_out, "w_o": w_o, "w_fg": w_fg, "w_fu": w_fu,
            "w_fd": w_fd, "w_pg": w_pg, "w_pp": w_pp}
    wname = {id(v): k for k, v in wsrc.items()}
    wconv = {}

    conv_gate = [[]]

    def convert_rest():
        if prelist:
            for key in prelist[7:]:
                convert(key)

    def convert(key):
        if key in wconv:
            return wconv[key]
        nm, k0, nk, c0, nc_ = key
        sc = nc.dram_tensor("wb%d" % len(wconv), [128, nk * nc_], BF16, kind="Internal").ap()
        scv = sc.rearrange("p (k c) -> p k c", c=nc_)
        b = Buf("wconv")
        s_ap = wsrc[nm][k0 * 128:(k0 + nk) * 128, c0:c0 + nc_].rearrange("(kc p) c -> p kc c", p=128)
        op("pool", lambda e: e.dma_start(out=scv, in_=s_ap), r=conv_gate[0], w=[b], dma=b)
        wconv[key] = (scv, b)
        return wconv[key]

    xbf = nc.dram_tensor("xbf", [NPRE + SEG, D], BF16, kind="Internal").ap()
    B_xconv = [Buf("xconv%d" % u) for u in range(16)]

    def convert_x(u):
        op("pool", lambda e: e.dma_start(out=xbf[u * 512:(u + 1) * 512, :], in_=xseg[u * 512:(u + 1) * 512, :]),
           w=[B_xconv[u]], dma=B_xconv[u])

    def wload(src, k0, nk, c0, nc_):
        key = (wname[id(src)], k0, nk, c0, nc_)
        if key not in wlist:
            wlist.append(key)
        scv, bconv = convert(key)
        ap, b = wslots[wi[0]]
        wi[0] = (wi[0] + 1) % NWB
        view = ap[:, 0:nk, 0:nc_]
        op("sp", lambda e: e.dma_start(out=view, in_=scv), r=[bconv], w=[b], dma=b)
        if copy_every[0]:
            copy_cnt[0] += 1
            if copy_cnt[0] % copy_every[0] == 0 and cache_copies:
                op("sp", cache_copies.pop(0), dma=B_cp)
        return view, b

    def wload2(srcA, srcB, k0, nk, c0, nc_):
        ap, b = wslots[wi[0]]
        wi[0] = (wi[0] + 1) % NWB
        for i_, src in enumerate((srcA, srcB)):
            key = (wname[id(src)], k0, nk, c0, nc_)
            if key not in wlist:
                wlist.append(key)
            scv, bconv = convert(key)
            view = ap[:, 0:nk, i_ * nc_:(i_ + 1) * nc_]
            op("sp", lambda e, view=view, scv=scv: e.dma_start(out=view, in_=scv), r=[bconv], w=[b], dma=b)
        return ap[:, 0:nk, 0:2 * nc_], b

    if prelist:
        nA = 4
        for key in prelist[:nA]:
            convert(key)
        for u in range(4):
            convert_x(u)
        for key in prelist[nA:nA + 3]:
            convert(key)
        for u in range(4, 16):
            convert_x(u)
    else:
        for u in range(16):
            convert_x(u)

    AP_ = Arena(nc, stack, "persist", 4200)
    S_f, B_Sf = AP_.alloc([128, 4, 256], F32)
    S_b, B_Sb = AP_.alloc([128, 4, 256], BF16)
    oAT, B_oAT = AP_.alloc([128, 2, SEG], BF16)
    op("dve", lambda e: e.memset(S_f, 0.0), w=[B_Sf])
    op("dve", lambda e: e.memset(S_b, 0.0), w=[B_Sb])

    AM = Arena(nc, stack, "main", 37200)

    evac_rr = [0]

    def evac(out, in_, r, w):
        evac_rr[0] = (evac_rr[0] + 1) % 3
        if evac_rr[0]:
            return op("act", lambda e: e.copy(out=out, in_=in_), r=r, w=w)
        return op("dve", lambda e: e.tensor_copy(out=out, in_=in_), r=r, w=w)

    def mm(out, lhsT, rhs, start, stop, r, w):
        return op("pe", lambda e: e.matmul(out, lhsT=lhsT, rhs=rhs, start=start, stop=stop), r=r, w=w)

    def load_xT(tok0, xT, B_xT, xb, B_xb):
        src = xbf[tok0:tok0 + 512, :].rearrange("(t p) d -> p t d", p=128)
        op("sp", lambda e: e.dma_start(out=xb, in_=src), r=[B_xconv[tok0 // 512]], w=[B_xb], dma=B_xb)
        for kc in range(8):
            pt, pb = bank()
            pv = pt[:, :].bitcast(BF16)
            for t in range(4):
                op("pe", lambda e, t=t, kc=kc, pv=pv: e.transpose(pv[:, t * 128:(t + 1) * 128],
                                                                   xb[:, t, kc * 128:(kc + 1) * 128], ident_b),
                   r=[B_xb, B_c], w=[pb])
            evac(xT[:, kc, :], pv[:, 0:512], r=[], w=[pb, B_xT])

    def gla_gates(xT, B_xT, wg, B_wg, Lt, B_L, glrT, B_glr, zt, B_z):
        pt, pb = bank()
        for kc in range(8):
            mm(pt[0:16, :], wg[:, kc, 0:16], xT[:, kc, :], kc == 0, kc == 7, r=[B_wg, B_xT], w=[pb])
        evac(glrT, pt[0:16, :], r=[], w=[pb, B_glr])
        for t in range(4):
            pt, pb = bank()
            mm(pt[:, :], glrT[:, t * 128:(t + 1) * 128], wgu_b, True, True, r=[B_glr, B_c], w=[pb])
            op("dve", lambda e, pt=pt, t=t: e.tensor_tensor(out=zt[:, t, :], in0=pt[:, :], in1=bgate_bc, op=ALU.add),
               r=[B_c], w=[pb, B_z])
        op("act", lambda e: e.activation(out=zt, in_=zt, func=AF.Exp, scale=-1.0), w=[B_z])
        op("act", lambda e: e.activation(out=Lt, in_=zt, func=AF.Ln, bias=one_c), r=[B_z, B_c], w=[B_L])


    G = {}

    def alloc_x():
        G["xb2"] = [AM.alloc([128, 4, 1024], BF16) for _ in range(2)]
        G["xT2"] = [AM.alloc([128, 8, 512], BF16) for _ in range(2)]

    def alloc_gla(own):
        names = dict(Lt=([128, 4, 512], F32), zt=([128, 4, 512], F32), glrT=([16, 512], BF16),
                     cumT=([128, 4, 512], F32), keT=([128, 4, 512], F32),
                     kdT=([128, 4, 512], BF16), kd=([128, 4, 4, 128], BF16), glast=([128, 4, 4], F32),
                     vgb=([128, 4, 1024], BF16))
        if own:
            names.update(qeT=([128, 4, 512], BF16), keTb=([128, 4, 512], BF16),
                         attb=([128, 4, 128], BF16))
        GG = {}
        for k_, (shp, dt_) in names.items():
            GG[k_] = AM.alloc(shp, dt_)
        GG["eB"] = GG["zt"]
        GG["eA"] = GG["Lt"]
        G.update(GG)
        return GG

    def gla_super(xT, B_xT, own, o_cb, GG=None, part="all"):
        GG = G if GG is None else GG
        Lt, B_L = GG["Lt"]; zt, B_z = GG["zt"]; glrT, B_glr = GG["glrT"]; cumT, B_cum = GG["cumT"]
        eB, B_eB = GG["eB"]; keT, B_keT = GG["keT"]; kdT, B_kdT = GG["kdT"]; kd, B_kd = GG["kd"]
        glast, B_gl = GG["glast"]; vgb, B_vg = GG["vgb"]
        if own:
            eA, B_eA = GG["eA"]; qeT, B_qeT = GG["qeT"]; keTb, B_keTb = GG["keTb"]; attb, B_attb = GG["attb"]
        if part != "back":
            gla_front(xT, B_xT, own, GG)
        if part != "front":
            gla_back(own, o_cb, GG)

    def gla_front(xT, B_xT, own, GG):
        Lt, B_L = GG["Lt"]; zt, B_z = GG["zt"]; glrT, B_glr = GG["glrT"]; cumT, B_cum = GG["cumT"]
        eB, B_eB = GG["eB"]; keT, B_keT = GG["keT"]; kdT, B_kdT = GG["kdT"]; kd, B_kd = GG["kd"]
        glast, B_gl = GG["glast"]; vgb, B_vg = GG["vgb"]
        if own:
            eA, B_eA = GG["eA"]; qeT, B_qeT = GG["qeT"]; keTb, B_keTb = GG["keTb"]; attb, B_attb = GG["attb"]
        wgl, B_wgl = wload(w_in, 0, 8, C_GLR, 16)
        gla_gates(xT, B_xT, wgl, B_wgl, Lt, B_L, glrT, B_glr, zt, B_z)
        for h in range(4):
            pt, pb = bank()
            for t in range(4):
                mm(pt[:, t * 128:(t + 1) * 128], Lt[:, t, h * 128:(h + 1) * 128], triu_f, True, True,
                   r=[B_L, B_c], w=[pb])
            evac(cumT[:, h, :], pt[:, :], r=[], w=[pb, B_cum])
        op("act", lambda e: e.activation(out=eB, in_=cumT, func=AF.Exp, scale=1.0 / 16.0), r=[B_cum], w=[B_eB])
        op("act", lambda e: e.activation(out=glast, in_=cumT[:, :, 127:512:128], func=AF.Exp, scale=-1.0 / 16.0),
           r=[B_cum], w=[B_gl])
        if own:
            op("act", lambda e: e.activation(out=eA, in_=cumT, func=AF.Exp, scale=-1.0 / 16.0), r=[B_cum], w=[B_eA])
        wkg, B_wkg = wload(w_in, 0, 8, C_KG, 512)
        for h in range(4):
            pt, pb = bank()
            for kc in range(8):
                mm(pt[:, :], wkg[:, kc, h * 128:(h + 1) * 128], xT[:, kc, :], kc == 0, kc == 7, r=[B_wkg, B_xT], w=[pb])
            op("dve", lambda e, pt=pt, h=h: e.tensor_tensor(out=keT[:, h, :], in0=pt[:, :], in1=eB[:, h, :], op=ALU.mult),
               r=[B_eB], w=[pb, B_keT])
        op("dve", lambda e: e.tensor_tensor(
            out=kdT[:, :, :].rearrange("p h (t q) -> p (h t) q", q=128),
            in0=keT[:, :, :].rearrange("p h (t q) -> p (h t) q", q=128),
            in1=glast[:, :, :].rearrange("p h t -> p (h t)").unsqueeze(2).broadcast_to([128, 16, 128]),
            op=ALU.mult), r=[B_keT, B_gl], w=[B_kdT])
        if own:
            wqg, B_wqg = wload(w_in, 0, 8, C_QG, 512)
            op("pool", lambda e: e.tensor_copy(out=keTb, in_=keT), r=[B_keT], w=[B_keTb])
            for h in range(4):
                pt, pb = bank()
                for kc in range(8):
                    mm(pt[:, :], wqg[:, kc, h * 128:(h + 1) * 128], xT[:, kc, :], kc == 0, kc == 7,
                       r=[B_wqg, B_xT], w=[pb])
                op("dve", lambda e, pt=pt, h=h: e.scalar_tensor_tensor(
                    out=qeT[:, h, :], in0=pt[:, :], scalar=float(128 ** -0.5), in1=eA[:, h, :],
                    op0=ALU.mult, op1=ALU.mult), r=[B_eA], w=[pb, B_qeT])
        wvg0, B_wvg0 = wload(w_in, 0, 8, C_VG, 512)
        wvg1, B_wvg1 = wload(w_in, 0, 8, C_VG + 512, 512)
        for t in range(4):
            for hf, (wv_, B_wv) in enumerate(((wvg0, B_wvg0), (wvg1, B_wvg1))):
                pt, pb = bank()
                for kc in range(8):
                    mm(pt[:, :], xT[:, kc, t * 128:(t + 1) * 128], wv_[:, kc, :], kc == 0, kc == 7,
                       r=[B_xT, B_wv], w=[pb])
                evac(vgb[:, t, hf * 512:(hf + 1) * 512], pt[:, :], r=[], w=[pb, B_vg])
        for t in range(4):
            pt, pb = bank()
            pv = pt[:, :].bitcast(BF16)
            for h in range(4):
                op("pe", lambda e, h=h, t=t, pv=pv: e.transpose(pv[:, h * 128:(h + 1) * 128],
                                                               kdT[:, h, t * 128:(t + 1) * 128], ident_b),
                   r=[B_kdT, B_c], w=[pb])
            evac(kd[:, t, :, :], pv[:, 0:512].rearrange("p (h d) -> p h d", d=128), r=[], w=[pb, B_kd])

    def gla_back(own, o_cb, GG):
        Lt, B_L = GG["Lt"]; zt, B_z = GG["zt"]; glrT, B_glr = GG["glrT"]; cumT, B_cum = GG["cumT"]
        eB, B_eB = GG["eB"]; keT, B_keT = GG["keT"]; kdT, B_kdT = GG["kdT"]; kd, B_kd = GG["kd"]
        glast, B_gl = GG["glast"]; vgb, B_vg = GG["vgb"]
        if own:
            eA, B_eA = GG["eA"]; qeT, B_qeT = GG["qeT"]; keTb, B_keTb = GG["keTb"]; attb, B_attb = GG["attb"]
        for t in range(4):
            if own:
                pa, pab = bank()
                for h in range(4):
                    mm(pa[:, h * 128:(h + 1) * 128], keTb[:, h, t * 128:(t + 1) * 128], qeT[:, h, t * 128:(t + 1) * 128],
                       True, True, r=[B_keTb, B_qeT], w=[pab])
                op("dve", lambda e, pa=pa: e.tensor_tensor(
                    out=attb, in0=pa[:, :].rearrange("p (h i) -> p h i", i=128),
                    in1=maskP[:, 128:256].unsqueeze(1).broadcast_to([128, 4, 128]), op=ALU.mult),
                   r=[B_c], w=[pab, B_attb])
                po = [bank(), bank()]
                for h in range(4):
                    pt, pb = po[h // 2]
                    c0 = (h % 2) * 256
                    mm(pt[:, c0:c0 + 256], qeT[:, h, t * 128:(t + 1) * 128], S_b[:, h, :], True, False,
                       r=[B_qeT, B_Sb], w=[pb])
                    mm(pt[:, c0:c0 + 256], attb[:, h, :], vgb[:, t, h * 256:(h + 1) * 256], False, True,
                       r=[B_attb, B_vg], w=[pb])
            ps_ = [bank(), bank()]
            for h in range(4):
                pt, pb = ps_[h // 2]
                c0 = (h % 2) * 256
                mm(pt[:, c0:c0 + 256], kd[:, t, h, :], vgb[:, t, h * 256:(h + 1) * 256], True, True,
                   r=[B_kd, B_vg], w=[pb])
            for h in range(4):
                pt, pb = ps_[h // 2]
                c0 = (h % 2) * 256
                op("dve", lambda e, pt=pt, c0=c0, h=h, t=t: e.scalar_tensor_tensor(
                    out=S_f[:, h, :], in0=S_f[:, h, :], scalar=glast[:, h, t:t + 1], in1=pt[:, c0:c0 + 256],
                    op0=ALU.mult, op1=ALU.add), r=[B_gl], w=[pb, B_Sf])
            if own:
                op("act", lambda e: e.copy(out=S_b, in_=S_f), r=[B_Sf], w=[B_Sb])
                o_cb(t, po)


    B_cp = Buf("cachecopy")
    cache_copies = []
    for n in range(NS):
        for r0, r1 in ((0, 512), (512, 1024), (1024, 1536), (1536, 2047)):
            cache_copies.append(lambda e, n=n, r0=r0, r1=r1: e.dma_start(out=kvs3[n, r0:r1, :], in_=c3[n, r0 + 1:r1 + 1, :]))
    for n in range(NS):
        cache_copies.append(lambda e, n=n: e.dma_start(out=kvs2[n, 0:511, :], in_=c2[n, 1:512, :]))
    cache_copies.append(lambda e: e.dma_start(out=kvs1[:, 0:127, :], in_=c1[:, 1:128, :]))
    copy_every = [0]
    copy_cnt = [0]

    def release_copies(k, B_dep):
        for _ in range(k):
            if cache_copies:
                op("sp", cache_copies.pop(0), r=[B_dep], dma=B_cp)

    PHASES.append(('A', {e_: len(S.q[e_]) for e_ in ENGS}))
    AM.reset()
    alloc_x()
    GA = [alloc_gla(False), alloc_gla(False)]
    def front_A(u):
        xb, B_xb = G["xb2"][u % 2]
        xT, B_xT = G["xT2"][u % 2]
        load_xT(u * 512, xT, B_xT, xb, B_xb)
        gla_super(xT, B_xT, False, None, GA[u % 2], part="front")

    front_A(0)
    for u in range(12):
        if u + 1 < 12:
            front_A(u + 1)
        gla_super(None, None, False, None, GA[u % 2], part="back")
    op("act", lambda e: e.copy(out=S_b, in_=S_f), r=[B_Sf], w=[B_Sb])

    PHASES.append(('B', {e_: len(S.q[e_]) for e_ in ENGS}))
    S.barrier()
    AM.reset()
    conv_gate[0] = [B_Sf]
    convert_rest()
    conv_gate[0] = []
    K1T, B_K1 = AM.alloc([128, 2, 17 * 128], BF16)
    K2T, B_K2 = AM.alloc([128, 2, 4, 5 * 128], BF16)
    K3T, B_K3 = AM.alloc([128, 2, 16, 2 * 128], BF16)
    Q1T, B_Q1 = AM.alloc([128, 2, 16 * 128], BF16)
    Q2T, B_Q2 = AM.alloc([128, 2, 4, 4 * 128], BF16)
    Q3T, B_Q3 = AM.alloc([128, 2, 16, 128], BF16)
    V1, B_V1 = AM.alloc([128, 17, 256], BF16)
    V2, B_V2 = AM.alloc([128, 4, 5, 256], BF16)
    V3, B_V3 = AM.alloc([128, 16, 2, 256], BF16)
    mark_B = AM.off
    alloc_x()
    qkf2 = [AM.alloc([128, 768], F32) for _ in range(2)]
    qkb2 = [AM.alloc([128, 768], BF16) for _ in range(2)]
    rt2 = [AM.alloc([128, 4, 12, 8], F32) for _ in range(2)]
    vst2 = [AM.alloc([128, 768], F32) for _ in range(2)]
    v3n2 = [AM.alloc([128, 256], BF16) for _ in range(2)]
    v3s = nc.dram_tensor("v3s", [2 * SEG, 256], BF16, kind="Internal").ap()
    B_v3s = Buf("v3s")
    v3i = [0]

    def v3_store(pt, row0):
        vn, B_vn = v3n2[v3i[0] % 2]
        v3i[0] += 1
        evac(vn, pt[:, 0:256], r=[], w=[pt_buf[0], B_vn])
        op("sp", lambda e: e.dma_start(out=v3s[row0:row0 + 128, :], in_=vn), w=[B_vn, B_v3s], dma=B_vn)

    pt_buf = [None]
    rr = [0]

    def a_proj(xT, B_xT, t, kind, groups, wv, B_w, ropetile, dsts, kvdst):
        i = rr[0] = rr[0] ^ 1
        qf, B_qf = qkf2[i]
        qb, B_qb = qkb2[i]
        rt, B_rt = rt2[i]
        ng = len(groups)
        ncol = 256 * ng
        g0 = groups[0]
        banks_ = []
        for c0 in range(0, ncol, 512):
            cw = min(512, ncol - c0)
            pt, pb = bank()
            for kc in range(8):
                mm(pt[:, 0:cw], xT[:, kc, t * 128:(t + 1) * 128], wv[:, kc, g0 * 256 + c0:g0 * 256 + c0 + cw],
                   kc == 0, kc == 7, r=[B_xT, B_w], w=[pb])
            op("act", lambda e, pt=pt, c0=c0, cw=cw: e.copy(out=qf[:, c0:c0 + cw], in_=pt[:, 0:cw]), w=[pb, B_qf])
        nh = 4 * ng
        qv = qf[:, 0:ncol].rearrange("p (h e) -> p h e", e=64)
        x1 = qv[:, :, 0:8]
        x2 = qv[:, :, 8:16]
        cs = ropet[:, ropetile, 0:8].unsqueeze(1).broadcast_to([128, nh, 8])
        sn = ropet[:, ropetile, 8:16].unsqueeze(1).broadcast_to([128, nh, 8])
        for k_, (a, b_) in enumerate(((x1, cs), (x2, sn), (x2, cs), (x1, sn))):
            op("dve", lambda e, k_=k_, a=a, b_=b_: e.tensor_tensor(out=rt[:, k_, 0:nh, :], in0=a, in1=b_, op=ALU.mult),
               r=[B_qf, B_c], w=[B_rt])
        op("dve", lambda e: e.tensor_tensor(out=x1, in0=rt[:, 0, 0:nh, :], in1=rt[:, 1, 0:nh, :], op=ALU.subtract),
           r=[B_rt], w=[B_qf])
        op("dve", lambda e: e.tensor_tensor(out=x2, in0=rt[:, 2, 0:nh, :], in1=rt[:, 3, 0:nh, :], op=ALU.add),
           r=[B_rt], w=[B_qf])
        if kvdst is not None:
            op("sp", lambda e: e.dma_start(out=kvdst, in_=qf[:, 0:768].rearrange("p (h e) -> p h e", e=64)),
               w=[B_qf], dma=B_qf)
        op("pool", lambda e: e.tensor_copy(out=qb[:, 0:ncol], in_=qf[:, 0:ncol]), r=[B_qf], w=[B_qb])

        def back():
            pt, pb = bank()
            pv = pt[:, :].bitcast(BF16)
            for c in range(2 * ng):
                op("pe", lambda e, c=c: e.transpose(pv[:, c * 128:(c + 1) * 128], qb[:, c * 128:(c + 1) * 128], ident_b),
                   r=[B_qb, B_c], w=[pb])
            for gi, g in enumerate(groups):
                src = pv[:, gi * 256:(gi + 1) * 256].rearrange("p (j q) -> p j q", j=2)
                dst, B_d = dsts[g]
                if DIL[g] > 1:
                    src = src.rearrange("p j (i r) -> p j r i", r=DIL[g])
                evac(dst, src, r=[], w=[pb, B_d])
        return back


    for u in range(8, 16):
        own = u >= 12
        s_ = u - 12
        hs = u - 8
        xb, B_xb = G["xb2"][u % 2]
        xT, B_xT = G["xT2"][u % 2]
        load_xT(u * 512, xT, B_xT, xb, B_xb)

        if True:
            kgroups = [0, 1, 2] if own else ([0, 1, 2] if u == 11 else [2])
            g0 = kgroups[0]
            wk, B_wk = wload(w_in, 0, 8, C_KA, 768)
            wv_, B_wv = wload(w_in, 0, 8, C_VA, 768)
            if own:
                wq, B_wq = wload(w_in, 0, 8, C_QA, 768)
            for t in range(4):
                tg = [g for g in kgroups if not (g == 0 and (not own) and t != 3)]
                ropetile = (16 + 4 * s_ + t) if own else (4 * hs + t)
                T = 4 * s_ + t
                kd_ = {}
                if 0 in tg:
                    blk = (1 + T) if own else 0
                    kd_[0] = (K1T[:, :, blk * 128:(blk + 1) * 128], B_K1)
                if 1 in tg:
                    blk = (1 + s_) if own else 0
                    kd_[1] = (K2T[:, :, :, blk * 128 + 32 * t: blk * 128 + 32 * t + 32], B_K2)
                sp_ = s_ if own else hs
                blk3 = 1 if own else 0
                kd_[2] = (K3T[:, :, :, blk3 * 128 + 32 * sp_ + 8 * t: blk3 * 128 + 32 * sp_ + 8 * t + 8], B_K3)
                kvd = kv_own[T * 128:(T + 1) * 128, 0, :, :] if own else None
                backs = [a_proj(xT, B_xT, t, "k", tg, wk, B_wk, ropetile, kd_, kvd)]
                if own:
                    qd_ = {0: (Q1T[:, :, T * 128:(T + 1) * 128], B_Q1),
                           1: (Q2T[:, :, :, s_ * 128 + 32 * t: s_ * 128 + 32 * t + 32], B_Q2),
                           2: (Q3T[:, :, :, 32 * s_ + 8 * t: 32 * s_ + 8 * t + 8], B_Q3)}
                    backs.append(a_proj(xT, B_xT, t, "q", [0, 1, 2], wq, B_wq, ropetile, qd_, None))
                if own or (u == 11 and t == 3):
                    vs, B_vs = vst2[t % 2]
                    ncol = 768 if own else 256
                    for c0 in range(0, ncol, 512):
                        cw = min(512, ncol - c0)
                        pt, pb = bank()
                        for kc in range(8):
                            mm(pt[:, 0:cw], xT[:, kc, t * 128:(t + 1) * 128], wv_[:, kc, c0:c0 + cw],
                               kc == 0, kc == 7, r=[B_xT, B_wv], w=[pb])
                        if c0 == 0:
                            blk = (1 + T) if own else 0
                            op("dve", lambda e, pt=pt, blk=blk: e.tensor_copy(out=V1[:, blk, :], in_=pt[:, 0:256]),
                               w=[pb, B_V1])
                        if own:
                            op("act", lambda e, pt=pt, c0=c0, cw=cw, vs=vs: e.copy(out=vs[:, c0:c0 + cw], in_=pt[:, 0:cw]),
                               w=[pb, B_vs])
                            if c0 == 512:
                                pt_buf[0] = pb
                                v3_store(pt, SEG + T * 128)
                    if own:
                        op("sp", lambda e, vs=vs, T=T: e.dma_start(
                            out=kv_own[T * 128:(T + 1) * 128, 1, :, :], in_=vs[:, :].rearrange("p (h e) -> p h e", e=64)),
                           w=[B_vs], dma=B_vs)
                for bk in backs:
                    bk()
            if own or u == 11:
                blk = (1 + s_) if own else 0
                for r_ in range(4):
                    pt, pb = bank()
                    for kc in range(8):
                        mm(pt[:, 0:256], xT[:, kc, r_:512:4], wv_[:, kc, 256:512], kc == 0, kc == 7,
                           r=[B_xT, B_wv], w=[pb])
                    evac(V2[:, r_, blk, :], pt[:, 0:256], r=[], w=[pb, B_V2])
            if not own:
                for t in range(4):
                    pt, pb = bank()
                    for kc in range(8):
                        mm(pt[:, 0:256], xT[:, kc, t * 128:(t + 1) * 128], wv_[:, kc, 512:768], kc == 0, kc == 7,
                           r=[B_xT, B_wv], w=[pb])
                    pt_buf[0] = pb
                    v3_store(pt, hs * 512 + t * 128)

    for blk in range(2):
        op("sp", lambda e, blk=blk: e.dma_start(
            out=V3[:, :, blk, :], in_=v3s[blk * SEG:(blk + 1) * SEG, :].rearrange("(i r) c -> i r c", r=16)),
           r=[B_v3s], w=[B_V3], dma=B_V3)

    PHASES.append(('C', {e_: len(S.q[e_]) for e_ in ENGS}))
    S.barrier()
    AM.off = mark_B
    release_copies(16, B_c)
    copy_every[0] = 2
    acc, B_acc = AM.alloc([128, 2, 2, SEG], F32)
    PT2 = [AM.alloc([128, 4, 256], BF16) for _ in range(2)]
    PM2 = [AM.alloc([128, 4, 256], BF16) for _ in range(2)]
    KT = (K1T, K2T, K3T); QT = (Q1T, Q2T, Q3T); VT = (V1, V2, V3)
    B_K = (B_K1, B_K2, B_K3); B_Q = (B_Q1, B_Q2, B_Q3); B_V = (B_V1, B_V2, B_V3)
    it = 0
    import os
    for g in [int(c_) for c_ in os.environ.get('KCG', '012')]:
        d = DIL[g]
        nblk = SEG // (128 * d)
        for r_ in range(d):
            for blk in range(1, nblk + 1):
                PT, B_PT = PT2[it % 2]
                PM, B_PM = PM2[it % 2]
                it += 1
                psS = [bank(), bank()]
                for h in range(4):
                    j_ = h // 2
                    p0 = 64 * (h % 2)
                    pt, pb = psS[h % 2]
                    for kbi, kb in enumerate((blk - 1, blk)):
                        if g == 0:
                            kv_ = K1T[p0:p0 + 64, j_, kb * 128:(kb + 1) * 128]
                            qv_ = Q1T[p0:p0 + 64, j_, (blk - 1) * 128:blk * 128]
                        else:
                            kv_ = KT[g][p0:p0 + 64, j_, r_, kb * 128:(kb + 1) * 128]
                            qv_ = QT[g][p0:p0 + 64, j_, r_, (blk - 1) * 128:blk * 128]
                        c0 = (h // 2) * 256 + kbi * 128
                        mm(pt[:, c0:c0 + 128], kv_, qv_, True, True, r=[B_K[g], B_Q[g]], w=[pb])
                for hp in range(2):
                    pt, pb = psS[hp]
                    op("act", lambda e, pt=pt, hp=hp, PT=PT: e.activation(
                        out=PT[:, hp:4:2, :], in_=pt[:, :].rearrange("p (h k) -> p h k", k=256),
                        func=AF.Exp, scale=0.125), w=[pb, B_PT])
                op("dve", lambda e, PT=PT, PM=PM: e.tensor_tensor(
                    out=PM, in0=PT, in1=maskP.unsqueeze(1).broadcast_to([128, 4, 256]), op=ALU.mult),
                   r=[B_PT, B_c], w=[B_PM])
                pN, pNb = bank()
                for h in range(4):
                    j_ = h // 2
                    p0 = 64 * (h % 2)
                    for nd in range(2):
                        for kbi, kb in enumerate((blk - 1, blk)):
                            if nd == 0:
                                lt_ = V1[:, kb, h * 64:(h + 1) * 64] if g == 0 else VT[g][:, r_, kb, h * 64:(h + 1) * 64]
                            else:
                                lt_ = hval_b if kb == 0 else ones_b
                            c0 = (nd * 2 + j_) * 128
                            mm(pN[p0:p0 + 64, c0:c0 + 128], lt_, PM[:, h, kbi * 128:(kbi + 1) * 128],
                               kbi == 0, kbi == 1, r=[B_V[g], B_PM, B_c], w=[pNb])
                tok0 = d * 128 * (blk - 1) + r_
                av = acc[:, :, :, tok0:tok0 + d * 127 + 1:d]
                pv4 = pN[:, :].rearrange("p (n j q) -> p n j q", n=2, j=2)
                if g == 0:
                    op("act", lambda e, av=av, pv4=pv4: e.copy(out=av, in_=pv4), w=[pNb, B_acc])
                else:
                    op("dve", lambda e, av=av, pv4=pv4: e.tensor_tensor(out=av, in0=av, in1=pv4, op=ALU.add),
                       w=[pNb, B_acc])
    if os.environ.get('KCF', '1') == '1':
        op("dve", lambda e: e.reciprocal(out=acc[:, 1, :, :], in_=acc[:, 1, :, :]), w=[B_acc])
        op("dve", lambda e: e.tensor_tensor(out=oAT, in0=acc[:, 0, :, :], in1=acc[:, 1, :, :], op=ALU.mult),
           r=[B_acc], w=[B_oAT])

    if os.environ.get("KSTOP") == "C":
        op("sp", lambda e: e.dma_start(out=st_own.rearrange("h d v -> d h v"), in_=S_f), w=[B_Sf], dma=B_Sf)
        S.barrier()
        S.op("sp", None)
        S.emit()
        return
    PHASES.append(('D', {e_: len(S.q[e_]) for e_ in ENGS}))
    S.barrier()
    AM.reset()
    vbc, B_vbc = AM.alloc([128, 5, 1024], F32)
    for i_ in range(5):
        op("sp", lambda e, i_=i_: e.dma_start(out=vbc[:, i_, :], in_=vecs[i_:i_ + 1, :].broadcast_to([128, D])),
           w=[B_vbc], dma=B_vbc)
    gng_bc = vbc[:, 0, :]
    xb, B_xb = AM.alloc([128, 4, 1024], BF16)
    xT, B_xT = AM.alloc([128, 8, 512], BF16)
    mark_gla = AM.off
    alloc_gla(True)
    end_gla = AM.off
    xres, B_xr = AM.alloc([128, 4, 1024], F32)
    rsil, B_rs = AM.alloc([128, 4, 1024], BF16)
    ob, B_ob = xb, B_xb
    obT, B_obT = AM.alloc([128, 8, 512], BF16)
    mgT, B_mg = AM.alloc([128, 8, 512], BF16)
    tmpA, B_tA = AM.alloc([128, 1024], F32)
    tmpB, B_tB = AM.alloc([128, 512], F32)
    ppb, B_ppb = AM.alloc([128, 4, 256], BF16)
    ppT, B_ppT = AM.alloc([128, 2, 512], BF16)
    st6, _ = AM.alloc([128, 4, 2, 6], F32)
    mv, _ = AM.alloc([128, 4, 4], F32)
    B_stt = [Buf('st%d' % i_) for i_ in range(4)]
    B_mvt = [Buf('mv%d' % i_) for i_ in range(4)]
    ss, B_ss = AM.alloc([128, 8], F32)
    save = AM.off
    AM.off = mark_gla
    hT, B_hT = AM.alloc([128, 22, 512], BF16)
    assert AM.off <= end_gla
    AM.off = save
    GLA_BUFS = list({id(v[1]): v[1] for k_, v in G.items() if not k_.startswith('x')}.values())

    def transposes_from_xb(NT, which=None):
        for kc in range(8):
            pt, pb = bank()
            pv = pt[:, :].bitcast(BF16)
            for t in range(NT):
                op("pe", lambda e, t=t, kc=kc, pv=pv: e.transpose(pv[:, t * 128:(t + 1) * 128],
                                                                   xb[:, t, kc * 128:(kc + 1) * 128], ident_b),
                   r=[B_xb, B_c], w=[pb])
            if which is None:
                evac(xT[:, kc, 0:NT * 128], pv[:, 0:NT * 128], r=[], w=[pb, B_xT])
            elif kc % 2 == 0:
                op("act", lambda e, kc=kc, pv=pv: e.activation(
                    out=xT[:, kc, 0:NT * 128], in_=pv[:, 0:NT * 128], func=AF.Identity,
                    scale=lncol[:, 2 * which, kc:kc + 1], bias=lncol[:, 2 * which + 1, kc:kc + 1]), r=[B_c], w=[pb, B_xT])
            else:
                op("dve", lambda e, kc=kc, pv=pv: e.tensor_scalar(
                    out=xT[:, kc, 0:NT * 128], in0=pv[:, 0:NT * 128], scalar1=lncol[:, 2 * which, kc:kc + 1],
                    scalar2=lncol[:, 2 * which + 1, kc:kc + 1], op0=ALU.mult, op1=ALU.add), r=[B_c], w=[pb, B_xT])

    def layer_norm(which, NT):
        g_bc = vbc[:, 1 + 2 * which, :]
        b_bc = vbc[:, 2 + 2 * which, :]
        for t in range(NT):
            B_st, B_mv = B_stt[t], B_mvt[t]
            for hf in range(2):
                op("dve", lambda e, t=t, hf=hf: e.bn_stats(out=st6[:, t, hf, :], in_=xres[:, t, hf * 512:(hf + 1) * 512]),
                   r=[B_xr], w=[B_st])
            op("dve", lambda e, t=t: e.bn_aggr(out=mv[:, t, 0:2], in_=st6[:, t, :, :].rearrange("p a b -> p (a b)")),
               r=[B_st], w=[B_mv])
            op("act", lambda e, t=t: e.activation(out=mv[:, t, 2:3], in_=mv[:, t, 1:2], func=AF.Sqrt, bias=eps_ln),
               r=[B_c], w=[B_mv])
            op("dve", lambda e, t=t: e.reciprocal(out=mv[:, t, 2:3], in_=mv[:, t, 2:3]), w=[B_mv])
            op("dve", lambda e, t=t: e.scalar_tensor_tensor(out=mv[:, t, 3:4], in0=mv[:, t, 0:1], scalar=-1.0,
                                                            in1=mv[:, t, 2:3], op0=ALU.mult, op1=ALU.mult), w=[B_mv])
            op("act", lambda e, t=t: e.activation(out=xb[:, t, :], in_=xres[:, t, :], func=AF.Identity,
                                                  scale=mv[:, t, 2:3], bias=mv[:, t, 3:4]), r=[B_mv, B_xr], w=[B_xb])
        for t in range(NT):
            B_mv = B_mvt[t]
            op("pool", lambda e, t=t: e.tensor_scalar(out=xres[:, t, :], in0=xres[:, t, :], scalar1=mv[:, t, 2:3],
                                                      scalar2=mv[:, t, 3:4], op0=ALU.mult, op1=ALU.add),
               r=[B_mv], w=[B_xr])
            op("pool", lambda e, t=t: e.tensor_tensor(out=xres[:, t, :], in0=xres[:, t, :], in1=g_bc, op=ALU.mult),
               r=[B_vbc], w=[B_xr])
            op("pool", lambda e, t=t: e.tensor_tensor(out=xres[:, t, :], in0=xres[:, t, :], in1=b_bc, op=ALU.add),
               r=[B_vbc], w=[B_xr])

    def o_cb(t, po):
        for hp in range(2):
            pt, pb = po[hp]
            op("act", lambda e, pt=pt, hp=hp: e.activation(out=tmpA[:, hp * 512:(hp + 1) * 512], in_=pt[:, :],
                                                           func=AF.Square), w=[pb, B_tA])
        op("dve", lambda e: e.tensor_reduce(out=ss[:, 0:4], in_=tmpA[:, :].rearrange("p (h v) -> p h v", v=256),
                                            axis=AX.X, op=ALU.add), r=[B_tA], w=[B_ss])
        op("act", lambda e: e.activation(out=ss[:, 4:8], in_=ss[:, 0:4], func=AF.Sqrt, scale=1.0 / 256.0,
                                         bias=eps_rms), r=[B_c], w=[B_ss])
        op("dve", lambda e: e.reciprocal(out=ss[:, 4:8], in_=ss[:, 4:8]), w=[B_ss])
        for h in range(4):
            pt, pb = po[h // 2]
            c0 = (h % 2) * 256
            op("dve", lambda e, pt=pt, c0=c0, h=h, t=t: e.scalar_tensor_tensor(
                out=ob[:, t, h * 256:(h + 1) * 256], in0=pt[:, c0:c0 + 256], scalar=ss[:, 4 + h:5 + h],
                in1=rsil[:, t, h * 256:(h + 1) * 256], op0=ALU.mult, op1=ALU.mult), r=[B_ss, B_rs], w=[pb, B_ob])

    def dense(NT, gla_fn, oA_fn, y_store):
        NK = NT * 128
        for hf in range(2):
            wr, B_wr = wload(w_in, 0, 8, C_RG + hf * 512, 512)
            for t in range(NT):
                pt, pb = bank()
                for kc in range(8):
                    mm(pt[:, :], xT[:, kc, t * 128:(t + 1) * 128], wr[:, kc, :], kc == 0, kc == 7, r=[B_xT, B_wr], w=[pb])
                op("act", lambda e, pt=pt: e.activation(out=tmpB, in_=pt[:, :], func=AF.Silu), w=[pb, B_tB])
                op("dve", lambda e, t=t, hf=hf: e.tensor_tensor(
                    out=rsil[:, t, hf * 512:(hf + 1) * 512], in0=tmpB, in1=gng_bc[:, hf * 512:(hf + 1) * 512], op=ALU.mult),
                   r=[B_tB, B_vbc], w=[B_rs])
        gla_fn()
        for t in range(NT):
            pt, pb = bank()
            pv = pt[:, :].bitcast(BF16)
            for kc in range(8):
                op("pe", lambda e, t=t, kc=kc, pv=pv: e.transpose(pv[:, kc * 128:(kc + 1) * 128],
                                                                   ob[:, t, kc * 128:(kc + 1) * 128], ident_b),
                   r=[B_ob, B_c], w=[pb])
            evac(obT[:, :, t * 128:(t + 1) * 128], pv[:, :].rearrange("p (k q) -> p k q", q=128), r=[], w=[pb, B_obT])
        for pas in range(2):
            for cg in range(2):
                if pas == 0:
                    wp_, B_wp = wload(w_a_out, 0, 2, cg * 512, 512)
                    wg_, B_wg_ = wload(w_in, 0, 8, C_GA + cg * 512, 512)
                    nk_ = 2
                else:
                    wp_, B_wp = wload(w_b_out, 0, 8, cg * 512, 512)
                    wg_, B_wg_ = wload(w_in, 0, 8, C_GB + cg * 512, 512)
                    nk_ = 8
                for c in range(4):
                    ch = cg * 4 + c
                    pg, pgb = bank()
                    for kc in range(8):
                        mm(pg[:, 0:NK], wg_[:, kc, c * 128:(c + 1) * 128], xT[:, kc, 0:NK], kc == 0, kc == 7,
                           r=[B_wg_, B_xT], w=[pgb])
                    op("act", lambda e, pg=pg: e.activation(out=tmpB[:, 0:NK], in_=pg[:, 0:NK], func=AF.Sigmoid),
                       w=[pgb, B_tB])
                    pp_, ppb_ = bank()
                    for kc in range(nk_):
                        if pas == 0:
                            rhs_, rb = oA_fn(kc)
                        else:
                            rhs_ = obT[:, kc, 0:NK]
                            rb = B_obT
                        mm(pp_[:, 0:NK], wp_[:, kc, c * 128:(c + 1) * 128], rhs_, kc == 0, kc == nk_ - 1, r=[B_wp, rb], w=[ppb_])
                    if pas == 0:
                        op("dve", lambda e, pp_=pp_, ch=ch: e.tensor_tensor(out=mgT[:, ch, 0:NK], in0=tmpB[:, 0:NK],
                                                                          in1=pp_[:, 0:NK], op=ALU.mult),
                           r=[B_tB], w=[ppb_, B_mg])
                    else:
                        op("dve", lambda e, pp_=pp_: e.tensor_tensor(out=tmpB[:, 0:NK], in0=tmpB[:, 0:NK], in1=pp_[:, 0:NK],
                                                                    op=ALU.mult), w=[ppb_, B_tB])
                        op("dve", lambda e, ch=ch: e.tensor_tensor(out=mgT[:, ch, 0:NK], in0=mgT[:, ch, 0:NK],
                                                                  in1=tmpB[:, 0:NK], op=ALU.add), r=[B_tB], w=[B_mg])
        for hf in range(2):
            wo_, B_wo = wload(w_o, 0, 8, hf * 512, 512)
            for t in range(NT):
                pt, pb = bank()
                for kc in range(8):
                    mm(pt[:, :], mgT[:, kc, t * 128:(t + 1) * 128], wo_[:, kc, :], kc == 0, kc == 7, r=[B_mg, B_wo], w=[pb])
                op("dve", lambda e, pt=pt, t=t, hf=hf: e.scalar_tensor_tensor(
                    out=xres[:, t, hf * 512:(hf + 1) * 512], in0=xres[:, t, hf * 512:(hf + 1) * 512], scalar=ALPHA,
                    in1=pt[:, :], op0=ALU.mult, op1=ALU.add), w=[pb, B_xr])
        layer_norm(0, NT)
        transposes_from_xb(NT, 0)
        if NT == 1:
            S.barrier()
        fc = 0
        for c0 in range(0, DFF, 256):
            wgu_, B_wgu = wload2(w_fg, w_fu, 0, 8, c0, 256)
            for c in range(2):
                pg, pgb = bank()
                pu, pub = bank()
                for kc in range(8):
                    mm(pg[:, 0:NK], wgu_[:, kc, c * 128:(c + 1) * 128], xT[:, kc, 0:NK], kc == 0, kc == 7, r=[B_wgu, B_xT], w=[pgb])
                for kc in range(8):
                    mm(pu[:, 0:NK], wgu_[:, kc, 256 + c * 128:256 + (c + 1) * 128], xT[:, kc, 0:NK], kc == 0, kc == 7,
                       r=[B_wgu, B_xT], w=[pub])
                op("act", lambda e, pg=pg: e.activation(out=tmpB[:, 0:NK], in_=pg[:, 0:NK], func=AF.Silu), w=[pgb, B_tB])
                op("dve", lambda e, pu=pu, fc=fc: e.tensor_tensor(out=hT[:, fc, 0:NK], in0=tmpB[:, 0:NK], in1=pu[:, 0:NK],
                                                                  op=ALU.mult), r=[B_tB], w=[pub, B_hT] + (GLA_BUFS if fc == 0 else []))
                fc += 1
        for hf in range(2):
            pbs = [bank() for _ in range(NT)]
            for f0, nf in ((0, 8), (8, 8), (16, 6)):
                wd_, B_wd = wload(w_fd, f0, nf, hf * 512, 512)
                for t in range(NT):
                    pt, pb = pbs[t]
                    for fi in range(nf):
                        mm(pt[:, :], hT[:, f0 + fi, t * 128:(t + 1) * 128], wd_[:, fi, :], f0 + fi == 0, f0 + fi == 21,
                           r=[B_hT, B_wd] + (GLA_BUFS if fi == nf - 1 else []), w=[pb])
            for t in range(NT):
                pt, pb = pbs[t]
                op("dve", lambda e, pt=pt, t=t, hf=hf: e.scalar_tensor_tensor(
                    out=xres[:, t, hf * 512:(hf + 1) * 512], in0=xres[:, t, hf * 512:(hf + 1) * 512], scalar=ALPHA,
                    in1=pt[:, :], op0=ALU.mult, op1=ALU.add), w=[pb, B_xr])
        layer_norm(1, NT)
        transposes_from_xb(NT, 1)
        for kc in range(2):
            pt, pb = bank()
            pv = pt[:, :].bitcast(BF16)
            for t in range(NT):
                op("pe", lambda e, t=t, kc=kc, pv=pv: e.transpose(pv[:, t * 128:(t + 1) * 128],
                                                                   ppb[:, t, kc * 128:(kc + 1) * 128], ident_b),
                   r=[B_ppb, B_c], w=[pb])
            evac(ppT[:, kc, 0:NK], pv[:, 0:NK], r=[], w=[pb, B_ppT])
        for hf in range(2):
            wg_, B_wg_ = wload(w_pg, 0, 8, hf * 512, 512)
            wp_, B_wp = wload(w_pp, 0, 2, hf * 512, 512)
            for t in range(NT):
                pg, pgb = bank()
                pp_, ppb_ = bank()
                for kc in range(8):
                    mm(pg[:, :], xT[:, kc, t * 128:(t + 1) * 128], wg_[:, kc, :], kc == 0, kc == 7, r=[B_xT, B_wg_], w=[pgb])
                for kc in range(2):
                    mm(pp_[:, :], ppT[:, kc, t * 128:(t + 1) * 128], wp_[:, kc, :], kc == 0, kc == 1, r=[B_ppT, B_wp], w=[ppb_])
                op("act", lambda e, pg=pg: e.activation(out=tmpB, in_=pg[:, :], func=AF.Sigmoid), w=[pgb, B_tB])
                op("dve", lambda e, pp_=pp_: e.tensor_tensor(out=tmpB, in0=tmpB, in1=pp_[:, :], op=ALU.mult),
                   w=[ppb_, B_tB])
                op("dve", lambda e, t=t, hf=hf: e.tensor_tensor(
                    out=xres[:, t, hf * 512:(hf + 1) * 512], in0=xres[:, t, hf * 512:(hf + 1) * 512], in1=tmpB, op=ALU.add),
                   r=[B_tB], w=[B_xr])
        y_store()

    for s_ in range(4):
        tok0 = NPRE + 512 * s_
        load_xT(tok0, xT, B_xT, xb, B_xb)
        op("sp", lambda e, tok0=tok0: e.dma_start(
            out=xres, in_=xseg[tok0:tok0 + 512, :].rearrange("(t p) d -> p t d", p=128)), w=[B_xr], dma=B_xr)
        op("pool", lambda e, s_=s_: e.dma_start(
            out=ppb, in_=pp[512 * s_:512 * s_ + 512, :].rearrange("(t p) d -> p t d", p=128)), w=[B_ppb], dma=B_ppb)
        dense(4, lambda: gla_super(xT, B_xT, True, o_cb),
              lambda kc, s_=s_: (oAT[:, kc, 512 * s_:512 * s_ + 512], B_oAT),
              lambda s_=s_: op("sp", lambda e: e.dma_start(
                  out=y_own[512 * s_:512 * s_ + 512, :].rearrange("(t p) d -> p t d", p=128), in_=xres),
                  w=[B_xr], dma=B_xr))

    op("sp", lambda e: e.dma_start(out=st_own.rearrange("h d v -> d h v"), in_=S_f), w=[B_Sf], dma=B_Sf)
    S.barrier()
    PHASES.append(('E', {e_: len(S.q[e_]) for e_ in ENGS}))
    NB[0] = 6
    psi[0] = 0
    save_off = AM.off
    AM.off = mark_gla
    NDEEP = 3
    hs, B_hs = AM.alloc([128, 4368], F32)
    rts, B_rts = AM.alloc([128, 4, 12, 8], F32)
    prod, B_prod = tmpA[:, 0:768], B_tA
    pnew, B_pn = AM.alloc([128, 24], F32)
    qTs, B_qTs = AM.alloc([128, 6, 16], F32)
    Qbd, B_Qbd = AM.alloc([128, 6, 16, 4], BF16)
    KV2 = [AM.alloc([128, 512], F32) for _ in range(NDEEP)]
    Vb2 = [AM.alloc([128, 512], BF16) for _ in range(NDEEP)]
    KTs2 = [AM.alloc([128, 256], BF16) for _ in range(NDEEP)]
    PTs2 = [AM.alloc([128, 4], BF16) for _ in range(NDEEP)]
    nds, B_nds = AM.alloc([128, 192], F32)
    ndtok = xres[:, 1:3, :].rearrange("p a b -> p (a b)")[:, 0:1536].rearrange("p (g c) -> p g c", c=512)
    B_ndt = Buf("ndtok")
    numt, B_numt = AM.alloc([128, 256], F32)
    dent, B_dent = AM.alloc([128, 256], F32)
    t1s, B_t1s = tmpB[:, 0:256], B_tB
    oasb, B_oasb = AM.alloc([128, 256], BF16)
    oATs, B_oATs = AM.alloc([128, 2, 128], BF16)
    glrTs, B_glrTs = AM.alloc([16, 128], BF16)
    Ls, B_Ls = AM.alloc([128, 512], F32)
    fm, B_fm = AM.alloc([128, 3, 4, 16], F32)
    ksel, B_ksel = AM.alloc([128, 16, 128], F32)
    qsel, B_qsel = AM.alloc([128, 4, 16, 16], F32)
    S02 = [AM.alloc([128, 256], F32) for _ in range(2)]
    Sn2 = [AM.alloc([128, 256], F32) for _ in range(2)]
    o_s, B_os = AM.alloc([128, 1024], F32)
    assert AM.off <= end_gla, (AM.off, end_gla)
    AM.off = save_off
    b6, B_b6 = psb[6]
    b7, B_b7 = psb[7]

    op("pool", lambda e: e.dma_start(out=xb[:, 0, :], in_=xs), w=[B_xb], dma=B_xb)
    op("sp", lambda e: e.dma_start(out=xres[:, 0, :], in_=xs), w=[B_xr], dma=B_xr)
    op("pool", lambda e: e.dma_start(out=ppb[:, 0, :], in_=pps), w=[B_ppb], dma=B_ppb)
    transposes_from_xb(1)
    op("dve", lambda e: e.memset(o_s, 0.0), w=[B_os])
    op("dve", lambda e: e.memset(oasb, 0.0), w=[B_oasb])
    for (c0, ncol, d0) in ((0, 768, 0), (768, 768, 768), (1536, 768, 1536), (2304, 512, 2304), (2816, 512, 2816),
                           (3328, 512, 3328), (3840, 512, 3840), (C_GLR, 16, 4352)):
        wv_, B_wv = wload(w_in, 0, 8, c0, ncol)
        for sub in range(0, ncol, 512):
            cw = min(512, ncol - sub)
            pt, pb = bank()
            for kc in range(8):
                mm(pt[:, 0:cw], xT[:, kc, 0:128], wv_[:, kc, sub:sub + cw], kc == 0, kc == 7, r=[B_xT, B_wv], w=[pb])
            evac(hs[:, d0 + sub:d0 + sub + cw], pt[:, 0:cw], r=[], w=[pb, B_hs])
    for qk in range(2):
        qv = hs[:, qk * 768:(qk + 1) * 768].rearrange("p (h e) -> p h e", e=64)
        x1 = qv[:, :, 0:8]
        x2 = qv[:, :, 8:16]
        cs = rope_s[:, 0:8].unsqueeze(1).broadcast_to([128, 12, 8])
        sn = rope_s[:, 8:16].unsqueeze(1).broadcast_to([128, 12, 8])
        for k_, (a, b_) in enumerate(((x1, cs), (x2, sn), (x2, cs), (x1, sn))):
            op("dve", lambda e, k_=k_, a=a, b_=b_: e.tensor_tensor(out=rts[:, k_, :, :], in0=a, in1=b_, op=ALU.mult),
               r=[B_hs, B_c], w=[B_rts])
        op("dve", lambda e, x1=x1: e.tensor_tensor(out=x1, in0=rts[:, 0, :, :], in1=rts[:, 1, :, :], op=ALU.subtract),
           r=[B_rts], w=[B_hs])
        op("dve", lambda e, x2=x2: e.tensor_tensor(out=x2, in0=rts[:, 2, :, :], in1=rts[:, 3, :, :], op=ALU.add),
           r=[B_rts], w=[B_hs])
    kvs_out = (kvs1, kvs2, kvs3)
    caches = (c1, c2, c3)
    WIN = (128, 512, 2048)
    for g in range(3):
        op("sp", lambda e, g=g: e.dma_start(out=kvs_out[g][:, WIN[g] - 1, 0:256], in_=hs[0:16, 768 + 256 * g:1024 + 256 * g]),
           w=[B_hs], dma=B_hs)
        op("sp", lambda e, g=g: e.dma_start(out=kvs_out[g][:, WIN[g] - 1, 256:512], in_=hs[0:16, 1536 + 256 * g:1792 + 256 * g]),
           w=[B_hs], dma=B_hs)
    op("dve", lambda e: e.tensor_tensor(out=prod, in0=hs[:, 0:768], in1=hs[:, 768:1536], op=ALU.mult), r=[B_hs], w=[B_prod])
    op("dve", lambda e: e.tensor_reduce(out=pnew[:, 0:12], in_=prod[:, :].rearrange("p (h e) -> p h e", e=64),
                                        axis=AX.X, op=ALU.add), r=[B_prod], w=[B_pn])
    op("act", lambda e: e.activation(out=pnew[:, 12:24], in_=pnew[:, 0:12], func=AF.Exp, scale=0.125), w=[B_pn])
    for c2_ in range(2):
        pt, pb = bank()
        for cc in range(3):
            c = c2_ * 3 + cc
            op("pe", lambda e, c=c, cc=cc, pt=pt: e.transpose(pt[:, cc * 128:(cc + 1) * 128], hs[:, c * 128:(c + 1) * 128], ident_f),
               r=[B_hs, B_c], w=[pb])
        evac(qTs[:, c2_ * 3:c2_ * 3 + 3, :], pt[:, 0:384].rearrange("p (c q) -> p c q", q=128)[:, :, 0:16], r=[], w=[pb, B_qTs])
    for c in range(6):
        op("dve", lambda e, c=c: e.tensor_tensor(
            out=Qbd[:, c, :, :], in0=qTs[:, c, :].unsqueeze(2).broadcast_to([128, 16, 4]),
            in1=bmask[:, c % 2, :].unsqueeze(1).broadcast_to([128, 16, 4]), op=ALU.mult), r=[B_qTs, B_c], w=[B_Qbd])
    it = 0
    for g in range(3):
        d = DIL[g]
        for n in range(NS):
            KV, B_KV = KV2[it % NDEEP]
            Vb, B_Vb = Vb2[it % NDEEP]
            KTs, B_KTs = KTs2[it % NDEEP]
            PTs, B_PTs = PTs2[it % NDEEP]
            it += 1
            op("sp", lambda e, g=g, n=n, d=d, KV=KV: e.dma_start(out=KV, in_=caches[g][n, 0:WIN[g]:d, :]), w=[B_KV], dma=B_KV)
            op("pool", lambda e, KV=KV, Vb=Vb: e.tensor_copy(out=Vb, in_=KV), r=[B_KV], w=[B_Vb])
            pt, pb = bank()
            pvb = pt[:, :].bitcast(BF16)
            for j_ in range(2):
                op("pe", lambda e, j_=j_, pvb=pvb, Vb=Vb: e.transpose(pvb[:, j_ * 128:(j_ + 1) * 128], Vb[:, j_ * 128:(j_ + 1) * 128], ident_b),
                   r=[B_Vb, B_c], w=[pb])
            evac(KTs, pvb[:, 0:256], r=[], w=[pb, B_KTs])
            ps_, psb_ = bank()
            for j_ in range(2):
                mm(ps_[:, 0:4], KTs[:, j_ * 128:(j_ + 1) * 128], Qbd[:, 2 * g + j_, n, :], j_ == 0, j_ == 1,
                   r=[B_KTs, B_Qbd], w=[psb_])
            op("act", lambda e, ps_=ps_, PTs=PTs: e.activation(out=PTs, in_=ps_[:, 0:4], func=AF.Exp, scale=0.125),
               w=[psb_, B_PTs])
            for h in range(4):
                p0 = 64 * (h % 2)
                for nd in range(2):
                    col = ((g * 2 + nd) * 2 + h // 2) * 16 + n
                    lt_ = Vb[:, 256 + h * 64:256 + (h + 1) * 64] if nd == 0 else ones_b
                    mm(b7[p0:p0 + 64, col:col + 1], lt_, PTs[:, h:h + 1], True, True, r=[B_Vb, B_PTs, B_c], w=[B_b7])
    op("act", lambda e: e.copy(out=nds, in_=b7[:, 0:192]), w=[B_b7, B_nds])
    for g in range(3):
        pt, pb = bank()
        for q_ in range(4):
            col = (g * 4 + q_) * 16
            op("pe", lambda e, q_=q_, col=col, pt=pt: e.transpose(pt[0:16, q_ * 128:(q_ + 1) * 128], nds[:, col:col + 16], ident_f),
               r=[B_nds, B_c], w=[pb])
        evac(ndtok[0:16, g, :], pt[0:16, :], r=[], w=[pb, B_ndt])
    R = slice(0, 16)
    op("dve", lambda e: e.tensor_tensor(out=numt[R, :], in0=ndtok[R, 0, 0:256], in1=ndtok[R, 1, 0:256], op=ALU.add), r=[B_ndt], w=[B_numt])
    op("dve", lambda e: e.tensor_tensor(out=numt[R, :], in0=numt[R, :], in1=ndtok[R, 2, 0:256], op=ALU.add), r=[B_ndt], w=[B_numt])
    op("dve", lambda e: e.tensor_tensor(out=dent[R, :], in0=ndtok[R, 0, 256:512], in1=ndtok[R, 1, 256:512], op=ALU.add), r=[B_ndt], w=[B_dent])
    op("dve", lambda e: e.tensor_tensor(out=dent[R, :], in0=dent[R, :], in1=ndtok[R, 2, 256:512], op=ALU.add), r=[B_ndt], w=[B_dent])
    for g in range(3):
        pb_ = pnew[R, 12 + 4 * g:16 + 4 * g].unsqueeze(2).broadcast_to([16, 4, 64])
        vn_ = hs[R, 1536 + 256 * g:1792 + 256 * g].rearrange("p (h e) -> p h e", e=64)
        op("dve", lambda e, pb_=pb_, vn_=vn_: e.tensor_tensor(out=t1s[R, :].rearrange("p (h e) -> p h e", e=64), in0=vn_, in1=pb_, op=ALU.mult),
           r=[B_pn, B_hs], w=[B_t1s])
        op("dve", lambda e: e.tensor_tensor(out=numt[R, :], in0=numt[R, :], in1=t1s[R, :], op=ALU.add), r=[B_t1s], w=[B_numt])
        op("dve", lambda e, pb_=pb_: e.tensor_tensor(out=dent[R, :].rearrange("p (h e) -> p h e", e=64),
                                                    in0=dent[R, :].rearrange("p (h e) -> p h e", e=64), in1=pb_, op=ALU.add),
           r=[B_pn], w=[B_dent])
    op("dve", lambda e: e.reciprocal(out=dent[R, :], in_=dent[R, :]), w=[B_dent])
    op("dve", lambda e: e.tensor_tensor(out=oasb[R, :], in0=numt[R, :], in1=dent[R, :], op=ALU.mult), r=[B_numt, B_dent], w=[B_oasb])
    pt, pb = bank()
    pv = pt[:, :].bitcast(BF16)
    for j_ in range(2):
        op("pe", lambda e, j_=j_, pv=pv: e.transpose(pv[:, j_ * 128:(j_ + 1) * 128], oasb[:, j_ * 128:(j_ + 1) * 128], ident_b),
           r=[B_oasb, B_c], w=[pb])
    evac(oATs, pv[:, 0:256].rearrange("p (j q) -> p j q", q=128), r=[], w=[pb, B_oATs])

    def sample_gla():
        pt, pb = bank()
        op("pe", lambda e, pt=pt: e.transpose(pt[0:16, 0:128], hs[:, 4352:4368], ident_f), r=[B_hs, B_c], w=[pb])
        evac(glrTs, pt[0:16, 0:128], r=[], w=[pb, B_glrTs])
        pt, pb = bank()
        mm(pt[:, :], glrTs, wgu_b, True, True, r=[B_glrTs, B_c], w=[pb])
        op("dve", lambda e, pt=pt: e.tensor_tensor(out=Ls, in0=pt[:, :], in1=bgate_bc, op=ALU.add), r=[B_c], w=[pb, B_Ls])
        op("act", lambda e: e.activation(out=Ls, in_=Ls, func=AF.Exp, scale=-1.0), w=[B_Ls])
        op("act", lambda e: e.activation(out=Ls, in_=Ls, func=AF.Ln, bias=one_c), r=[B_c], w=[B_Ls])
        op("act", lambda e: e.activation(out=Ls, in_=Ls, func=AF.Exp, scale=-1.0 / 16.0), w=[B_Ls])
        for wi_, src in enumerate((Ls, hs[:, 2816:3328], hs[:, 2304:2816])):
            pt, pb = bank()
            for h in range(4):
                op("pe", lambda e, h=h, pt=pt, src=src: e.transpose(pt[:, h * 128:(h + 1) * 128], src[:, h * 128:(h + 1) * 128], ident_f),
                   r=[B_Ls, B_hs, B_c], w=[pb])
            evac(fm[:, wi_, :, :], pt[:, :].rearrange("p (h q) -> p h q", q=128)[:, :, 0:16], r=[], w=[pb, B_fm])
        for h in range(4):
            op("dve", lambda e, h=h: e.scalar_tensor_tensor(
                out=qsel[:, h, :, :], in0=fm[:, 2, h, :].unsqueeze(2).broadcast_to([128, 16, 16]), scalar=float(128 ** -0.5),
                in1=eye16b[:, :].rearrange("p (a b) -> p a b", b=16), op0=ALU.mult, op1=ALU.mult), r=[B_fm, B_c], w=[B_qsel])
        i_ = 0
        for h in range(4):
            op("dve", lambda e, h=h: e.tensor_tensor(
                out=ksel[R, :, :], in0=hs[R, 2816 + h * 128:2944 + h * 128].unsqueeze(1).broadcast_to([16, 16, 128]),
                in1=ident_f[R, 0:16].unsqueeze(2).broadcast_to([16, 16, 128]), op=ALU.mult), r=[B_hs, B_c], w=[B_ksel])
            for n in range(NS):
                S0, B_S0 = S02[i_ % 2]
                Sn, B_Sn = Sn2[i_ % 2]
                i_ += 1
                op("sp", lambda e, n=n, h=h, S0=S0: e.dma_start(out=S0, in_=st_in[n, h, :, :]), w=[B_S0], dma=B_S0)
                ps_, psb_ = bank()
                mm(ps_[:, 0:256], ksel[R, n, :], hs[R, 3328 + h * 256:3584 + h * 256], True, True, r=[B_ksel, B_hs], w=[psb_])
                op("dve", lambda e, ps_=ps_, S0=S0, Sn=Sn, h=h, n=n: e.scalar_tensor_tensor(
                    out=Sn, in0=S0, scalar=fm[:, 0, h, n:n + 1], in1=ps_[:, 0:256], op0=ALU.mult, op1=ALU.add),
                   r=[B_S0, B_fm], w=[psb_, B_Sn])
                mm(b6[0:16, 0:256], qsel[:, h, n, :], Sn, n == 0, n == NS - 1, r=[B_qsel, B_Sn], w=[B_b6])
                op("sp", lambda e, n=n, h=h, Sn=Sn: e.dma_start(out=st_out[n, h, :, :], in_=Sn), w=[B_Sn], dma=B_Sn)
            op("act", lambda e, h=h: e.copy(out=o_s[R, h * 256:(h + 1) * 256], in_=b6[0:16, 0:256]), w=[B_b6, B_os])
        op("act", lambda e: e.activation(out=tmpA, in_=o_s, func=AF.Square), r=[B_os], w=[B_tA])
        op("dve", lambda e: e.tensor_reduce(out=ss[:, 0:4], in_=tmpA[:, :].rearrange("p (h v) -> p h v", v=256),
                                            axis=AX.X, op=ALU.add), r=[B_tA], w=[B_ss])
        op("act", lambda e: e.activation(out=ss[:, 4:8], in_=ss[:, 0:4], func=AF.Sqrt, scale=1.0 / 256.0,
                                         bias=eps_rms), r=[B_c], w=[B_ss])
        op("dve", lambda e: e.reciprocal(out=ss[:, 4:8], in_=ss[:, 4:8]), w=[B_ss])
        for h in range(4):
            op("dve", lambda e, h=h: e.scalar_tensor_tensor(
                out=ob[:, 0, h * 256:(h + 1) * 256], in0=o_s[:, h * 256:(h + 1) * 256], scalar=ss[:, 4 + h:5 + h],
                in1=rsil[:, 0, h * 256:(h + 1) * 256], op0=ALU.mult, op1=ALU.mult), r=[B_ss, B_rs, B_os], w=[B_ob])

    dense(1, sample_gla, lambda kc: (oATs[:, kc, :], B_oATs),
          lambda: op("sp", lambda e: e.dma_start(out=y_s, in_=xres[:, 0, :]), w=[B_xr], dma=B_xr))
    release_copies(999, B_c)
    S.barrier()
    S.op("sp", None)
    S.emit()


def _rope_tables():
    inv = (500000.0 ** (-np.arange(0, 16, 2, dtype=np.float32) / np.float32(16))).astype(np.float32)
    return inv


_PROG = None
TAGS = None
PHASES = []
_RAW = [False]


def kernel(x_prompt, x_sample, cache_a1_kv, cache_a2_kv, cache_a3_kv, state_gla, p_prompt, p_sample,
           w_in, w_gate_up, b_gate, gla_norm_g, w_a_out, w_b_out, w_o, ln1_g, ln1_b,
           w_ff_gate, w_ff_up, w_ff_down, ln2_g, ln2_b, w_ple_gate, w_ple_proj):
    global _PROG
    f = lambda a: np.ascontiguousarray(np.asarray(a, dtype=np.float32))
    x_prompt = f(x_prompt)
    if _PROG is None:
        _PROG = build_program()
    nc = _PROG
    ii = np.arange(128)
    cst = np.zeros((128, 640), np.float32)
    cst[:, 0:128] = np.eye(128, dtype=np.float32)
    cst[:, 128:256] = (ii[:, None] <= ii[None, :])
    cst[:, 256:384] = (ii[:, None] >= ii[None, :])
    cst[:, 384:448] = 1.0
    vecs = np.zeros((6, D), np.float32)
    vecs[0] = f(gla_norm_g)[0]; vecs[1] = f(ln1_g)[0]; vecs[2] = f(ln1_b)[0]
    vecs[3] = f(ln2_g)[0]; vecs[4] = f(ln2_b)[0]; vecs[5, 0:512] = f(b_gate)[0]
    inv = _rope_tables()
    shared = dict(w_in=f(w_in)[0], w_gu=f(w_gate_up)[0], vecs=vecs, w_a_out=f(w_a_out)[0], w_b_out=f(w_b_out)[0],
                  w_o=f(w_o)[0], w_fg=f(w_ff_gate)[0], w_fu=f(w_ff_up)[0], w_fd=f(w_ff_down)[0],
                  w_pg=f(w_ple_gate)[0], w_pp=f(w_ple_proj)[0])
    x_sample = f(x_sample); p_sample = f(p_sample); state_gla = f(state_gla)
    cache_a1_kv = f(cache_a1_kv); cache_a2_kv = f(cache_a2_kv); cache_a3_kv = f(cache_a3_kv)
    ang_s = np.float32(SEQ) * inv
    rope_s = np.tile(np.concatenate([np.cos(ang_s), np.sin(ang_s)]).astype(np.float32)[None, :], (128, 1))
    cst2 = np.zeros((128, 296), np.float32)
    for i_ in range(4):
        cst2[:, 264 + 8 * i_:272 + 8 * i_] = vecs[1 + i_].reshape(8, 128).T
    cst2[:, 0:256] = np.eye(16, dtype=np.float32).reshape(1, 256)
    for p_ in range(128):
        for j_ in range(2):
            cst2[p_, 256 + j_ * 4 + 2 * j_ + p_ // 64] = 1.0
    in_maps = []
    for c in range(NCORES):
        b, j = divmod(c, 4)
        t0 = j * SEG
        xs = np.zeros((NPRE + SEG, D), np.float32)
        lo = t0 - NPRE
        src_lo = max(lo, 0)
        xs[src_lo - lo:] = x_prompt[b, src_lo:t0 + SEG]
        pos = np.maximum(np.arange(t0 - SEG, t0 + SEG), 0).astype(np.float32)
        ang = pos[:, None] * inv[None, :]
        tab = np.concatenate([np.cos(ang), np.sin(ang)], axis=1).astype(np.float32)
        tab = np.ascontiguousarray(tab.reshape(32, 128, 16).transpose(1, 0, 2))
        cc = cst.copy()
        cc[:, 512:576] = 1.0 if j > 0 else 0.0
        m = dict(shared)
        m.update(xseg=xs, pp=f(p_prompt)[0, b, t0:t0 + SEG], rope=tab, cst=cc)
        n0 = NS * c
        xsp = np.zeros((128, D), np.float32)
        xsp[:NS] = x_sample[n0:n0 + NS, 0]
        ppp = np.zeros((128, 256), np.float32)
        ppp[:NS] = p_sample[0, n0:n0 + NS, 0]
        m.update(xs=xsp, pps=ppp, rope_s=rope_s, cst2=cst2,
                 c1=cache_a1_kv[0, n0:n0 + NS].reshape(NS, 128, 512),
                 c2=cache_a2_kv[0, n0:n0 + NS].reshape(NS, 512, 512),
                 c3=cache_a3_kv[0, n0:n0 + NS].reshape(NS, 2048, 512),
                 st_in=state_gla[0, n0:n0 + NS])
        in_maps.append(m)
    res = run_bass_kernel_spmd(nc, in_maps, core_ids=list(range(NCORES))).results
    if _RAW[0]:
        return res
    y_prompt = np.stack([np.concatenate([res[4 * b + j]["y_own"] for j in range(4)], axis=0) for b in range(2)])
    kvp = []
    for g, keep in enumerate((128, 512, 2048)):
        kvp.append(np.stack([res[4 * b + 3]["kv_own"][SEG - keep:, :, 4 * g:4 * g + 4, :] for b in range(2)])[None])
    st_p = np.stack([res[4 * b + 3]["st_own"] for b in range(2)])[None]
    y_sample = np.concatenate([res[c]["y_s"][:NS] for c in range(NCORES)], axis=0)[:, None, :]
    kvs = []
    for g, (nm, w_) in enumerate((("kvs1", 128), ("kvs2", 512), ("kvs3", 2048))):
        kvs.append(np.concatenate([res[c][nm] for c in range(NCORES)], axis=0).reshape(1, 128, w_, 2, 4, 64))
    st_s = np.concatenate([res[c]["st_out"] for c in range(NCORES)], axis=0)[None]
    return (y_prompt.astype(np.float32), y_sample, kvp[0], kvp[1], kvp[2], st_p,
            kvs[0], kvs[1], kvs[2], st_s)
```

```python
import contextlib
import numpy as np
import concourse.bass as bass
import concourse.mybir as mybir
from concourse.bass_utils import run_bass_kernel_spmd

F32 = mybir.dt.float32
BF16 = mybir.dt.bfloat16
AF = mybir.ActivationFunctionType
ALU = mybir.AluOpType
AX = mybir.AxisListType

NCORES = 8
D = 1024
SEQ = 8192
SEG = 2048
NPRE = 3 * SEG
A_W = 768
DFF = 2816
IN_COLS = 7440
C_QA, C_KA, C_VA, C_QG, C_KG, C_VG, C_RG, C_GLR, C_GA, C_GB = 0, 768, 1536, 2304, 2816, 3328, 4352, 5376, 5392, 6416
DIL = (1, 4, 16)
ALPHA = float(2.0 ** 0.25)
LN_EPS = 1e-5
RMS_EPS = 1e-6
NS = 16
ENGS = ("pe", "act", "dve", "pool", "sp")


class Buf:
    __slots__ = ("name", "lw", "rd", "dsem", "dcnt")

    def __init__(self, name):
        self.name = name
        self.lw = None
        self.rd = []
        self.dsem = None
        self.dcnt = 0


class Ins:
    __slots__ = ("eng", "fn", "deps", "needed", "ticket", "is_dma", "dsem", "dval", "idx", "tag")


class Sched:
    def __init__(self, nc, stack):
        self.nc = nc
        self.stack = stack
        self.q = {e: [] for e in ENGS}
        self.esem = {e: stack.enter_context(nc.semaphore("es_" + e)) for e in ENGS}
        self.nsem = 0
        self.bar = {e: None for e in ENGS}
        self.alldma = []

    def newsem(self):
        self.nsem += 1
        return self.stack.enter_context(self.nc.semaphore("ds%d" % self.nsem))

    def op(self, eng, fn, r=(), w=(), dma=None):
        ins = Ins()
        ins.eng = eng
        ins.fn = fn
        ins.needed = False
        ins.ticket = 0
        ins.is_dma = dma is not None
        ins.idx = len(self.q[eng])
        ins.tag = 0
        if TAGS is not None:
            import sys as _sys
            fr = _sys._getframe(1)
            while fr.f_code.co_name in ("mm", "evac", "op"):
                fr = fr.f_back
            ins.tag = fr.f_lineno
        deps = set()
        for b in r:
            if b.lw is not None:
                deps.add(b.lw)
        for b in w:
            if b.lw is not None:
                deps.add(b.lw)
            deps.update(b.rd)
        if self.bar[eng] is not None:
            deps.update(self.bar[eng])
            self.bar[eng] = None
        deps.discard(ins)
        best = {}
        for d in deps:
            if d.is_dma:
                key = ("d", id(d.dsem))
                val = d.dval
            else:
                if d.eng == "pe" and eng == "pe" and not ins.is_dma:
                    continue
                key = ("e", d.eng)
                val = d.idx
            if key not in best or best[key][0] < val:
                best[key] = (val, d)
        ins.deps = [v[1] for v in best.values()]
        if ins.is_dma:
            if dma.dsem is None:
                dma.dsem = self.newsem()
            dma.dcnt += 16
            ins.dsem = dma.dsem
            ins.dval = dma.dcnt
            self.alldma.append(ins)
        for b in w:
            b.lw = ins
            b.rd = []
        for b in r:
            b.rd.append(ins)
        self.q[eng].append(ins)
        return ins

    def barrier(self):
        deps = []
        for e in ENGS:
            if self.q[e]:
                deps.append(self.q[e][-1])
        last = {}
        for d in self.alldma:
            last[id(d.dsem)] = d
        deps.extend(last.values())
        for e in ENGS:
            self.bar[e] = list(deps)

    def emit(self):
        nc = self.nc
        for e in ENGS:
            for ins in self.q[e]:
                for d in ins.deps:
                    d.needed = True
        for e in ENGS:
            cnt = 0
            for ins in self.q[e]:
                if ins.needed and not ins.is_dma:
                    cnt += 1
                    ins.ticket = cnt
        esem = self.esem
        if TAGS is not None:
            for e in ENGS:
                TAGS[e] = [(ins.tag, ins.fn is not None, ins.is_dma) for ins in self.q[e]]

        def run(e, eo):
            seen = {}
            for ins in self.q[e]:
                waits = {}
                for d in ins.deps:
                    if d.is_dma:
                        key, sem, val = ("d", id(d.dsem)), d.dsem, d.dval
                    else:
                        key, sem, val = ("e", d.eng), esem[d.eng], d.ticket
                    if seen.get(key, 0) >= val:
                        continue
                    if key not in waits or waits[key][1] < val:
                        waits[key] = (sem, val)
                for key, (sem, val) in waits.items():
                    eo.wait_ge(sem, val)
                    seen[key] = val
                if ins.fn is None:
                    continue
                h = ins.fn(eo)
                if ins.is_dma:
                    h.then_inc(ins.dsem, 16)
                elif ins.needed:
                    h.then_inc(esem[e], 1)

        with nc.Block() as block:
            @block.tensor
            def _(eo):
                run("pe", eo)

            @block.scalar
            def _(eo):
                run("act", eo)

            @block.vector
            def _(eo):
                run("dve", eo)

            @block.gpsimd
            def _(eo):
                run("pool", eo)

            @block.sync
            def _(eo):
                run("sp", eo)


class Arena:
    def __init__(self, nc, stack, name, nwords):
        self.t = stack.enter_context(nc.sbuf_tensor(name, [128, nwords], F32))
        self.n = nwords
        self.off = 0
        self.name = name
        self.k = 0

    def reset(self):
        self.off = 0

    def alloc(self, shape, dtype):
        per = int(np.prod(shape[1:]))
        words = per if dtype == F32 else (per + 1) // 2
        words = (words + 7) // 8 * 8
        assert self.off + words <= self.n, (self.name, self.off, words, self.n)
        ap = self.t[0:shape[0], self.off:self.off + words]
        self.off += words
        if dtype != F32:
            ap = ap.bitcast(dtype)
        ap = ap[:, 0:per]
        if len(shape) == 3:
            ap = ap.rearrange("p (a b) -> p a b", b=shape[2])
        elif len(shape) == 4:
            ap = ap.rearrange("p (a b c) -> p a b c", b=shape[2], c=shape[3])
        self.k += 1
        return ap, Buf("%s_%d" % (self.name, self.k))


def build_program():
    wl = []
    nc = bass.Bass("TRN2", target_bir_lowering=False)
    with contextlib.ExitStack() as stack:
        _build(nc, stack, None, wl)
    del PHASES[:]
    nc = bass.Bass("TRN2", target_bir_lowering=False)
    with contextlib.ExitStack() as stack:
        _build(nc, stack, wl, [])
    return nc


def _build(nc, stack, prelist, wlist):
    def din(name, shape):
        return nc.dram_tensor(name, list(shape), F32, kind="ExternalInput").ap()

    def dout(name, shape):
        return nc.dram_tensor(name, list(shape), F32, kind="ExternalOutput").ap()

    xseg = din("xseg", (NPRE + SEG, D))
    pp = din("pp", (SEG, 256))
    rope = din("rope", (128, 32, 16))
    cst = din("cst", (128, 640))
    w_in = din("w_in", (D, IN_COLS))
    w_gu = din("w_gu", (16, 512))
    vecs = din("vecs", (6, D))
    w_a_out = din("w_a_out", (256, D))
    w_b_out = din("w_b_out", (D, D))
    w_o = din("w_o", (D, D))
    w_fg = din("w_fg", (D, DFF))
    w_fu = din("w_fu", (D, DFF))
    w_fd = din("w_fd", (DFF, D))
    w_pg = din("w_pg", (D, D))
    w_pp = din("w_pp", (256, D))

    xs = din("xs", (128, D))
    pps = din("pps", (128, 256))
    rope_s_d = din("rope_s", (128, 16))
    cst2 = din("cst2", (128, 296))
    c1 = din("c1", (NS, 128, 512))
    c2 = din("c2", (NS, 512, 512))
    c3 = din("c3", (NS, 2048, 512))
    st_in = din("st_in", (NS, 4, 128, 256))
    y_s = dout("y_s", (128, D))
    kvs1 = dout("kvs1", (NS, 128, 512))
    kvs2 = dout("kvs2", (NS, 512, 512))
    kvs3 = dout("kvs3", (NS, 2048, 512))
    st_out = dout("st_out", (NS, 4, 128, 256))
    y_own = dout("y_own", (SEG, D))
    kv_own = dout("kv_own", (SEG, 2, 12, 64))
    st_own = dout("st_own", (4, 128, 256))

    S = Sched(nc, stack)
    op = S.op

    psb = []
    for i in range(8):
        t = stack.enter_context(nc.psum_tensor("ps%d" % i, [128, 512], F32))
        psb.append((t, Buf("ps%d" % i)))
    psi = [0]
    NB = [8]

    def bank():
        i = psi[0]
        psi[0] = (i + 1) % NB[0]
        return psb[i]

    AC = Arena(nc, stack, "consts", 2400)
    ident_f, B_c = AC.alloc([128, 128], F32)
    triu_f, _ = AC.alloc([128, 128], F32)
    ident_b, _ = AC.alloc([128, 128], BF16)
    maskP, _ = AC.alloc([128, 256], BF16)
    ones_b, _ = AC.alloc([128, 64], BF16)
    hval_b, _ = AC.alloc([128, 64], BF16)
    ropet, _ = AC.alloc([128, 32, 16], F32)
    wgu_b, _ = AC.alloc([16, 512], BF16)
    eps_ln, _ = AC.alloc([128, 1], F32)
    eps_rms, _ = AC.alloc([128, 1], F32)
    one_c, _ = AC.alloc([128, 1], F32)
    B_c = Buf("consts")

    op("sp", lambda e: e.dma_start(out=ident_f, in_=cst[:, 0:128]), w=[B_c], dma=B_c)
    op("sp", lambda e: e.dma_start(out=triu_f, in_=cst[:, 128:256]), w=[B_c], dma=B_c)
    op("sp", lambda e: e.dma_start(out=ropet, in_=rope), w=[B_c], dma=B_c)
    op("pool", lambda e: e.dma_start(out=ident_b, in_=cst[:, 0:128]), w=[B_c], dma=B_c)
    op("pool", lambda e: e.dma_start(out=maskP[:, 0:128], in_=cst[:, 256:384]), w=[B_c], dma=B_c)
    op("pool", lambda e: e.dma_start(out=maskP[:, 128:256], in_=cst[:, 128:256]), w=[B_c], dma=B_c)
    op("pool", lambda e: e.dma_start(out=ones_b, in_=cst[:, 384:448]), w=[B_c], dma=B_c)
    op("pool", lambda e: e.dma_start(out=hval_b, in_=cst[:, 512:576]), w=[B_c], dma=B_c)
    op("pool", lambda e: e.dma_start(out=wgu_b, in_=w_gu), w=[B_c], dma=B_c)
    rope_s, _ = AC.alloc([128, 16], F32)
    eye16b, _ = AC.alloc([128, 256], F32)
    bmask, _ = AC.alloc([128, 2, 4], F32)
    lncol, _ = AC.alloc([128, 4, 8], F32)
    bgate_bc, _ = AC.alloc([128, 512], F32)
    op("sp", lambda e: e.dma_start(out=bgate_bc, in_=vecs[5:6, 0:512].broadcast_to([128, 512])), w=[B_c], dma=B_c)
    op("sp", lambda e: e.dma_start(out=rope_s, in_=rope_s_d), w=[B_c], dma=B_c)
    op("sp", lambda e: e.dma_start(out=eye16b, in_=cst2[:, 0:256]), w=[B_c], dma=B_c)
    op("sp", lambda e: e.dma_start(out=bmask, in_=cst2[:, 256:264].rearrange("p (a b) -> p a b", b=4)), w=[B_c], dma=B_c)
    op("sp", lambda e: e.dma_start(out=lncol, in_=cst2[:, 264:296].rearrange("p (a b) -> p a b", b=8)), w=[B_c], dma=B_c)
    op("dve", lambda e: e.memset(eps_ln, LN_EPS), w=[B_c])
    op("dve", lambda e: e.memset(eps_rms, RMS_EPS), w=[B_c])
    op("dve", lambda e: e.memset(one_c, 1.0), w=[B_c])

    NWB = 3
    AW = Arena(nc, stack, "wbuf", NWB * 3072)
    wslots = [AW.alloc([128, 8, 768], BF16) for _ in range(NWB)]
    wi = [0]
    wsrc = {"w_in": w_in, "w_a_out": w_a_out, "w_b_out": w_b_out, "w_o": w_o, "w_fg": w_fg, "w_fu": w_fu,
            "w_fd": w_fd, "w_pg": w_pg, "w_pp": w_pp}
    wname = {id(v): k for k, v in wsrc.items()}
    wconv = {}

    conv_gate = [[]]

    def convert_rest():
        if prelist:
            for key in prelist[7:]:
                convert(key)

    def convert(key):
        if key in wconv:
            return wconv[key]
        nm, k0, nk, c0, nc_ = key
        sc = nc.dram_tensor("wb%d" % len(wconv), [128, nk * nc_], BF16, kind="Internal").ap()
        scv = sc.rearrange("p (k c) -> p k c", c=nc_)
        b = Buf("wconv")
        s_ap = wsrc[nm][k0 * 128:(k0 + nk) * 128, c0:c0 + nc_].rearrange("(kc p) c -> p kc c", p=128)
        op("pool", lambda e: e.dma_start(out=scv, in_=s_ap), r=conv_gate[0], w=[b], dma=b)
        wconv[key] = (scv, b)
        return wconv[key]

    xbf = nc.dram_tensor("xbf", [NPRE + SEG, D], BF16, kind="Internal").ap()
    B_xconv = [Buf("xconv%d" % u) for u in range(16)]

    def convert_x(u):
        op("pool", lambda e: e.dma_start(out=xbf[u * 512:(u + 1) * 512, :], in_=xseg[u * 512:(u + 1) * 512, :]),
           w=[B_xconv[u]], dma=B_xconv[u])

    def wload(src, k0, nk, c0, nc_):
        key = (wname[id(src)], k0, nk, c0, nc_)
        if key not in wlist:
            wlist.append(key)
        scv, bconv = convert(key)
        ap, b = wslots[wi[0]]
        wi[0] = (wi[0] + 1) % NWB
        view = ap[:, 0:nk, 0:nc_]
        op("sp", lambda e: e.dma_start(out=view, in_=scv), r=[bconv], w=[b], dma=b)
        if copy_every[0]:
            copy_cnt[0] += 1
            if copy_cnt[0] % copy_every[0] == 0 and cache_copies:
                op("sp", cache_copies.pop(0), dma=B_cp)
        return view, b

    def wload2(srcA, srcB, k0, nk, c0, nc_):
        ap, b = wslots[wi[0]]
        wi[0] = (wi[0] + 1) % NWB
        for i_, src in enumerate((srcA, srcB)):
            key = (wname[id(src)], k0, nk, c0, nc_)
            if key not in wlist:
                wlist.append(key)
            scv, bconv = convert(key)
            view = ap[:, 0:nk, i_ * nc_:(i_ + 1) * nc_]
            op("sp", lambda e, view=view, scv=scv: e.dma_start(out=view, in_=scv), r=[bconv], w=[b], dma=b)
        return ap[:, 0:nk, 0:2 * nc_], b

    if prelist:
        nA = 4
        for key in prelist[:nA]:
            convert(key)
        for u in range(4):
            convert_x(u)
        for key in prelist[nA:nA + 3]:
            convert(key)
        for u in range(4, 16):
            convert_x(u)
    else:
        for u in range(16):
            convert_x(u)

    AP_ = Arena(nc, stack, "persist", 4200)
    S_f, B_Sf = AP_.alloc([128, 4, 256], F32)
    S_b, B_Sb = AP_.alloc([128, 4, 256], BF16)
    oAT, B_oAT = AP_.alloc([128, 2, SEG], BF16)
    op("dve", lambda e: e.memset(S_f, 0.0), w=[B_Sf])
    op("dve", lambda e: e.memset(S_b, 0.0), w=[B_Sb])

    AM = Arena(nc, stack, "main", 37200)

    evac_rr = [0]

    def evac(out, in_, r, w):
        evac_rr[0] ^= 1
        if evac_rr[0]:
            return op("act", lambda e: e.copy(out=out, in_=in_), r=r, w=w)
        return op("dve", lambda e: e.tensor_copy(out=out, in_=in_), r=r, w=w)

    def mm(out, lhsT, rhs, start, stop, r, w):
        return op("pe", lambda e: e.matmul(out, lhsT=lhsT, rhs=rhs, start=start, stop=stop), r=r, w=w)

    def load_xT(tok0, xT, B_xT, xb, B_xb):
        src = xbf[tok0:tok0 + 512, :].rearrange("(t p) d -> p t d", p=128)
        op("sp", lambda e: e.dma_start(out=xb, in_=src), r=[B_xconv[tok0 // 512]], w=[B_xb], dma=B_xb)
        for kc in range(8):
            pt, pb = bank()
            pv = pt[:, :].bitcast(BF16)
            for t in range(4):
                op("pe", lambda e, t=t, kc=kc, pv=pv: e.transpose(pv[:, t * 128:(t + 1) * 128],
                                                                   xb[:, t, kc * 128:(kc + 1) * 128], ident_b),
                   r=[B_xb, B_c], w=[pb])
            evac(xT[:, kc, :], pv[:, 0:512], r=[], w=[pb, B_xT])

    def gla_gates(xT, B_xT, wg, B_wg, Lt, B_L, glrT, B_glr, zt, B_z):
        pt, pb = bank()
        for kc in range(8):
            mm(pt[0:16, :], wg[:, kc, 0:16], xT[:, kc, :], kc == 0, kc == 7, r=[B_wg, B_xT], w=[pb])
        evac(glrT, pt[0:16, :], r=[], w=[pb, B_glr])
        for t in range(4):
            pt, pb = bank()
            mm(pt[:, :], glrT[:, t * 128:(t + 1) * 128], wgu_b, True, True, r=[B_glr, B_c], w=[pb])
            op("dve", lambda e, pt=pt, t=t: e.tensor_tensor(out=zt[:, t, :], in0=pt[:, :], in1=bgate_bc, op=ALU.add),
               r=[B_c], w=[pb, B_z])
        op("act", lambda e: e.activation(out=zt, in_=zt, func=AF.Exp, scale=-1.0), w=[B_z])
        op("act", lambda e: e.activation(out=Lt, in_=zt, func=AF.Ln, bias=one_c), r=[B_z, B_c], w=[B_L])


    G = {}

    def alloc_x():
        G["xb2"] = [AM.alloc([128, 4, 1024], BF16) for _ in range(2)]
        G["xT2"] = [AM.alloc([128, 8, 512], BF16) for _ in range(2)]

    def alloc_gla(own):
        names = dict(Lt=([128, 4, 512], F32), zt=([128, 4, 512], F32), glrT=([16, 512], BF16),
                     cumT=([128, 4, 512], F32), keT=([128, 4, 512], F32),
                     kdT=([128, 4, 512], BF16), kd=([128, 4, 4, 128], BF16), glast=([128, 4, 4], F32),
                     vgb=([128, 4, 1024], BF16))
        if own:
            names.update(qeT=([128, 4, 512], BF16), keTb=([128, 4, 512], BF16),
                         attb=([128, 4, 128], BF16))
        GG = {}
        for k_, (shp, dt_) in names.items():
            GG[k_] = AM.alloc(shp, dt_)
        GG["eB"] = GG["zt"]
        GG["eA"] = GG["Lt"]
        G.update(GG)
        return GG

    def gla_super(xT, B_xT, own, o_cb, GG=None, part="all"):
        GG = G if GG is None else GG
        Lt, B_L = GG["Lt"]; zt, B_z = GG["zt"]; glrT, B_glr = GG["glrT"]; cumT, B_cum = GG["cumT"]
        eB, B_eB = GG["eB"]; keT, B_keT = GG["keT"]; kdT, B_kdT = GG["kdT"]; kd, B_kd = GG["kd"]
        glast, B_gl = GG["glast"]; vgb, B_vg = GG["vgb"]
        if own:
            eA, B_eA = GG["eA"]; qeT, B_qeT = GG["qeT"]; keTb, B_keTb = GG["keTb"]; attb, B_attb = GG["attb"]
        if part != "back":
            gla_front(xT, B_xT, own, GG)
        if part != "front":
            gla_back(own, o_cb, GG)

    def gla_front(xT, B_xT, own, GG):
        Lt, B_L = GG["Lt"]; zt, B_z = GG["zt"]; glrT, B_glr = GG["glrT"]; cumT, B_cum = GG["cumT"]
        eB, B_eB = GG["eB"]; keT, B_keT = GG["keT"]; kdT, B_kdT = GG["kdT"]; kd, B_kd = GG["kd"]
        glast, B_gl = GG["glast"]; vgb, B_vg = GG["vgb"]
        if own:
            eA, B_eA = GG["eA"]; qeT, B_qeT = GG["qeT"]; keTb, B_keTb = GG["keTb"]; attb, B_attb = GG["attb"]
        wgl, B_wgl = wload(w_in, 0, 8, C_GLR, 16)
        gla_gates(xT, B_xT, wgl, B_wgl, Lt, B_L, glrT, B_glr, zt, B_z)
        for h in range(4):
            pt, pb = bank()
            for t in range(4):
                mm(pt[:, t * 128:(t + 1) * 128], Lt[:, t, h * 128:(h + 1) * 128], triu_f, True, True,
                   r=[B_L, B_c], w=[pb])
            evac(cumT[:, h, :], pt[:, :], r=[], w=[pb, B_cum])
        op("act", lambda e: e.activation(out=eB, in_=cumT, func=AF.Exp, scale=1.0 / 16.0), r=[B_cum], w=[B_eB])
        op("act", lambda e: e.activation(out=glast, in_=cumT[:, :, 127:512:128], func=AF.Exp, scale=-1.0 / 16.0),
           r=[B_cum], w=[B_gl])
        if own:
            op("act", lambda e: e.activation(out=eA, in_=cumT, func=AF.Exp, scale=-1.0 / 16.0), r=[B_cum], w=[B_eA])
        wkg, B_wkg = wload(w_in, 0, 8, C_KG, 512)
        for h in range(4):
            pt, pb = bank()
            for kc in range(8):
                mm(pt[:, :], wkg[:, kc, h * 128:(h + 1) * 128], xT[:, kc, :], kc == 0, kc == 7, r=[B_wkg, B_xT], w=[pb])
            op("dve", lambda e, pt=pt, h=h: e.tensor_tensor(out=keT[:, h, :], in0=pt[:, :], in1=eB[:, h, :], op=ALU.mult),
               r=[B_eB], w=[pb, B_keT])
        op("dve", lambda e: e.tensor_tensor(
            out=kdT[:, :, :].rearrange("p h (t q) -> p (h t) q", q=128),
            in0=keT[:, :, :].rearrange("p h (t q) -> p (h t) q", q=128),
            in1=glast[:, :, :].rearrange("p h t -> p (h t)").unsqueeze(2).broadcast_to([128, 16, 128]),
            op=ALU.mult), r=[B_keT, B_gl], w=[B_kdT])
        if own:
            wqg, B_wqg = wload(w_in, 0, 8, C_QG, 512)
            op("pool", lambda e: e.tensor_copy(out=keTb, in_=keT), r=[B_keT], w=[B_keTb])
            for h in range(4):
                pt, pb = bank()
                for kc in range(8):
                    mm(pt[:, :], wqg[:, kc, h * 128:(h + 1) * 128], xT[:, kc, :], kc == 0, kc == 7,
                       r=[B_wqg, B_xT], w=[pb])
                op("dve", lambda e, pt=pt, h=h: e.scalar_tensor_tensor(
                    out=qeT[:, h, :], in0=pt[:, :], scalar=float(128 ** -0.5), in1=eA[:, h, :],
                    op0=ALU.mult, op1=ALU.mult), r=[B_eA], w=[pb, B_qeT])
        wvg0, B_wvg0 = wload(w_in, 0, 8, C_VG, 512)
        wvg1, B_wvg1 = wload(w_in, 0, 8, C_VG + 512, 512)
        for t in range(4):
            for hf, (wv_, B_wv) in enumerate(((wvg0, B_wvg0), (wvg1, B_wvg1))):
                pt, pb = bank()
                for kc in range(8):
                    mm(pt[:, :], xT[:, kc, t * 128:(t + 1) * 128], wv_[:, kc, :], kc == 0, kc == 7,
                       r=[B_xT, B_wv], w=[pb])
                evac(vgb[:, t, hf * 512:(hf + 1) * 512], pt[:, :], r=[], w=[pb, B_vg])
        for t in range(4):
            pt, pb = bank()
            pv = pt[:, :].bitcast(BF16)
            for h in range(4):
                op("pe", lambda e, h=h, t=t, pv=pv: e.transpose(pv[:, h * 128:(h + 1) * 128],
                                                               kdT[:, h, t * 128:(t + 1) * 128], ident_b),
                   r=[B_kdT, B_c], w=[pb])
            evac(kd[:, t, :, :], pv[:, 0:512].rearrange("p (h d) -> p h d", d=128), r=[], w=[pb, B_kd])

    def gla_back(own, o_cb, GG):
        Lt, B_L = GG["Lt"]; zt, B_z = GG["zt"]; glrT, B_glr = GG["glrT"]; cumT, B_cum = GG["cumT"]
        eB, B_eB = GG["eB"]; keT, B_keT = GG["keT"]; kdT, B_kdT = GG["kdT"]; kd, B_kd = GG["kd"]
        glast, B_gl = GG["glast"]; vgb, B_vg = GG["vgb"]
        if own:
            eA, B_eA = GG["eA"]; qeT, B_qeT = GG["qeT"]; keTb, B_keTb = GG["keTb"]; attb, B_attb = GG["attb"]
        for t in range(4):
            if own:
                pa, pab = bank()
                for h in range(4):
                    mm(pa[:, h * 128:(h + 1) * 128], keTb[:, h, t * 128:(t + 1) * 128], qeT[:, h, t * 128:(t + 1) * 128],
                       True, True, r=[B_keTb, B_qeT], w=[pab])
                op("dve", lambda e, pa=pa: e.tensor_tensor(
                    out=attb, in0=pa[:, :].rearrange("p (h i) -> p h i", i=128),
                    in1=maskP[:, 128:256].unsqueeze(1).broadcast_to([128, 4, 128]), op=ALU.mult),
                   r=[B_c], w=[pab, B_attb])
                po = [bank(), bank()]
                for h in range(4):
                    pt, pb = po[h // 2]
                    c0 = (h % 2) * 256
                    mm(pt[:, c0:c0 + 256], qeT[:, h, t * 128:(t + 1) * 128], S_b[:, h, :], True, False,
                       r=[B_qeT, B_Sb], w=[pb])
                    mm(pt[:, c0:c0 + 256], attb[:, h, :], vgb[:, t, h * 256:(h + 1) * 256], False, True,
                       r=[B_attb, B_vg], w=[pb])
            ps_ = [bank(), bank()]
            for h in range(4):
                pt, pb = ps_[h // 2]
                c0 = (h % 2) * 256
                mm(pt[:, c0:c0 + 256], kd[:, t, h, :], vgb[:, t, h * 256:(h + 1) * 256], True, True,
                   r=[B_kd, B_vg], w=[pb])
            for h in range(4):
                pt, pb = ps_[h // 2]
                c0 = (h % 2) * 256
                op("dve", lambda e, pt=pt, c0=c0, h=h, t=t: e.scalar_tensor_tensor(
                    out=S_f[:, h, :], in0=S_f[:, h, :], scalar=glast[:, h, t:t + 1], in1=pt[:, c0:c0 + 256],
                    op0=ALU.mult, op1=ALU.add), r=[B_gl], w=[pb, B_Sf])
            if own:
                op("act", lambda e: e.copy(out=S_b, in_=S_f), r=[B_Sf], w=[B_Sb])
                o_cb(t, po)


    B_cp = Buf("cachecopy")
    cache_copies = []
    for n in range(NS):
        for r0, r1 in ((0, 512), (512, 1024), (1024, 1536), (1536, 2047)):
            cache_copies.append(lambda e, n=n, r0=r0, r1=r1: e.dma_start(out=kvs3[n, r0:r1, :], in_=c3[n, r0 + 1:r1 + 1, :]))
    for n in range(NS):
        cache_copies.append(lambda e, n=n: e.dma_start(out=kvs2[n, 0:511, :], in_=c2[n, 1:512, :]))
    cache_copies.append(lambda e: e.dma_start(out=kvs1[:, 0:127, :], in_=c1[:, 1:128, :]))
    copy_every = [0]
    copy_cnt = [0]

    def release_copies(k, B_dep):
        for _ in range(k):
            if cache_copies:
                op("sp", cache_copies.pop(0), r=[B_dep], dma=B_cp)

    PHASES.append(('A', {e_: len(S.q[e_]) for e_ in ENGS}))
    AM.reset()
    alloc_x()
    GA = [alloc_gla(False), alloc_gla(False)]
    def front_A(u):
        xb, B_xb = G["xb2"][u % 2]
        xT, B_xT = G["xT2"][u % 2]
        load_xT(u * 512, xT, B_xT, xb, B_xb)
        gla_super(xT, B_xT, False, None, GA[u % 2], part="front")

    front_A(0)
    for u in range(12):
        if u + 1 < 12:
            front_A(u + 1)
        gla_super(None, None, False, None, GA[u % 2], part="back")
    op("act", lambda e: e.copy(out=S_b, in_=S_f), r=[B_Sf], w=[B_Sb])

    PHASES.append(('B', {e_: len(S.q[e_]) for e_ in ENGS}))
    S.barrier()
    AM.reset()
    conv_gate[0] = [B_Sf]
    convert_rest()
    conv_gate[0] = []
    K1T, B_K1 = AM.alloc([128, 2, 17 * 128], BF16)
    K2T, B_K2 = AM.alloc([128, 2, 4, 5 * 128], BF16)
    K3T, B_K3 = AM.alloc([128, 2, 16, 2 * 128], BF16)
    Q1T, B_Q1 = AM.alloc([128, 2, 16 * 128], BF16)
    Q2T, B_Q2 = AM.alloc([128, 2, 4, 4 * 128], BF16)
    Q3T, B_Q3 = AM.alloc([128, 2, 16, 128], BF16)
    V1, B_V1 = AM.alloc([128, 17, 256], BF16)
    V2, B_V2 = AM.alloc([128, 4, 5, 256], BF16)
    V3, B_V3 = AM.alloc([128, 16, 2, 256], BF16)
    mark_B = AM.off
    alloc_x()
    qkf2 = [AM.alloc([128, 768], F32) for _ in range(2)]
    qkb2 = [AM.alloc([128, 768], BF16) for _ in range(2)]
    rt2 = [AM.alloc([128, 4, 12, 8], F32) for _ in range(2)]
    vst2 = [AM.alloc([128, 768], F32) for _ in range(2)]
    v3n2 = [AM.alloc([128, 256], BF16) for _ in range(2)]
    v3s = nc.dram_tensor("v3s", [2 * SEG, 256], BF16, kind="Internal").ap()
    B_v3s = Buf("v3s")
    v3i = [0]

    def v3_store(pt, row0):
        vn, B_vn = v3n2[v3i[0] % 2]
        v3i[0] += 1
        evac(vn, pt[:, 0:256], r=[], w=[pt_buf[0], B_vn])
        op("sp", lambda e: e.dma_start(out=v3s[row0:row0 + 128, :], in_=vn), w=[B_vn, B_v3s], dma=B_vn)

    pt_buf = [None]
    rr = [0]

    def a_proj(xT, B_xT, t, kind, groups, wv, B_w, ropetile, dsts, kvdst):
        i = rr[0] = rr[0] ^ 1
        qf, B_qf = qkf2[i]
        qb, B_qb = qkb2[i]
        rt, B_rt = rt2[i]
        ng = len(groups)
        ncol = 256 * ng
        g0 = groups[0]
        banks_ = []
        for c0 in range(0, ncol, 512):
            cw = min(512, ncol - c0)
            pt, pb = bank()
            for kc in range(8):
                mm(pt[:, 0:cw], xT[:, kc, t * 128:(t + 1) * 128], wv[:, kc, g0 * 256 + c0:g0 * 256 + c0 + cw],
                   kc == 0, kc == 7, r=[B_xT, B_w], w=[pb])
            op("act", lambda e, pt=pt, c0=c0, cw=cw: e.copy(out=qf[:, c0:c0 + cw], in_=pt[:, 0:cw]), w=[pb, B_qf])
        nh = 4 * ng
        qv = qf[:, 0:ncol].rearrange("p (h e) -> p h e", e=64)
        x1 = qv[:, :, 0:8]
        x2 = qv[:, :, 8:16]
        cs = ropet[:, ropetile, 0:8].unsqueeze(1).broadcast_to([128, nh, 8])
        sn = ropet[:, ropetile, 8:16].unsqueeze(1).broadcast_to([128, nh, 8])
        for k_, (a, b_) in enumerate(((x1, cs), (x2, sn), (x2, cs), (x1, sn))):
            op("dve", lambda e, k_=k_, a=a, b_=b_: e.tensor_tensor(out=rt[:, k_, 0:nh, :], in0=a, in1=b_, op=ALU.mult),
               r=[B_qf, B_c], w=[B_rt])
        op("dve", lambda e: e.tensor_tensor(out=x1, in0=rt[:, 0, 0:nh, :], in1=rt[:, 1, 0:nh, :], op=ALU.subtract),
           r=[B_rt], w=[B_qf])
        op("dve", lambda e: e.tensor_tensor(out=x2, in0=rt[:, 2, 0:nh, :], in1=rt[:, 3, 0:nh, :], op=ALU.add),
           r=[B_rt], w=[B_qf])
        if kvdst is not None:
            op("sp", lambda e: e.dma_start(out=kvdst, in_=qf[:, 0:768].rearrange("p (h e) -> p h e", e=64)),
               w=[B_qf], dma=B_qf)
        op("pool", lambda e: e.tensor_copy(out=qb[:, 0:ncol], in_=qf[:, 0:ncol]), r=[B_qf], w=[B_qb])

        def back():
            pt, pb = bank()
            pv = pt[:, :].bitcast(BF16)
            for c in range(2 * ng):
                op("pe", lambda e, c=c: e.transpose(pv[:, c * 128:(c + 1) * 128], qb[:, c * 128:(c + 1) * 128], ident_b),
                   r=[B_qb, B_c], w=[pb])
            for gi, g in enumerate(groups):
                src = pv[:, gi * 256:(gi + 1) * 256].rearrange("p (j q) -> p j q", j=2)
                dst, B_d = dsts[g]
                if DIL[g] > 1:
                    src = src.rearrange("p j (i r) -> p j r i", r=DIL[g])
                evac(dst, src, r=[], w=[pb, B_d])
        return back


    for u in range(8, 16):
        own = u >= 12
        s_ = u - 12
        hs = u - 8
        xb, B_xb = G["xb2"][u % 2]
        xT, B_xT = G["xT2"][u % 2]
        load_xT(u * 512, xT, B_xT, xb, B_xb)

        if True:
            kgroups = [0, 1, 2] if own else ([0, 1, 2] if u == 11 else [2])
            g0 = kgroups[0]
            wk, B_wk = wload(w_in, 0, 8, C_KA, 768)
            wv_, B_wv = wload(w_in, 0, 8, C_VA, 768)
            if own:
                wq, B_wq = wload(w_in, 0, 8, C_QA, 768)
            for t in range(4):
                tg = [g for g in kgroups if not (g == 0 and (not own) and t != 3)]
                ropetile = (16 + 4 * s_ + t) if own else (4 * hs + t)
                T = 4 * s_ + t
                kd_ = {}
                if 0 in tg:
                    blk = (1 + T) if own else 0
                    kd_[0] = (K1T[:, :, blk * 128:(blk + 1) * 128], B_K1)
                if 1 in tg:
                    blk = (1 + s_) if own else 0
                    kd_[1] = (K2T[:, :, :, blk * 128 + 32 * t: blk * 128 + 32 * t + 32], B_K2)
                sp_ = s_ if own else hs
                blk3 = 1 if own else 0
                kd_[2] = (K3T[:, :, :, blk3 * 128 + 32 * sp_ + 8 * t: blk3 * 128 + 32 * sp_ + 8 * t + 8], B_K3)
                kvd = kv_own[T * 128:(T + 1) * 128, 0, :, :] if own else None
                backs = [a_proj(xT, B_xT, t, "k", tg, wk, B_wk, ropetile, kd_, kvd)]
                if own:
                    qd_ = {0: (Q1T[:, :, T * 128:(T + 1) * 128], B_Q1),
                           1: (Q2T[:, :, :, s_ * 128 + 32 * t: s_ * 128 + 32 * t + 32], B_Q2),
                           2: (Q3T[:, :, :, 32 * s_ + 8 * t: 32 * s_ + 8 * t + 8], B_Q3)}
                    backs.append(a_proj(xT, B_xT, t, "q", [0, 1, 2], wq, B_wq, ropetile, qd_, None))
                if own or (u == 11 and t == 3):
                    vs, B_vs = vst2[t % 2]
                    ncol = 768 if own else 256
                    for c0 in range(0, ncol, 512):
                        cw = min(512, ncol - c0)
                        pt, pb = bank()
                        for kc in range(8):
                            mm(pt[:, 0:cw], xT[:, kc, t * 128:(t + 1) * 128], wv_[:, kc, c0:c0 + cw],
                               kc == 0, kc == 7, r=[B_xT, B_wv], w=[pb])
                        if c0 == 0:
                            blk = (1 + T) if own else 0
                            op("dve", lambda e, pt=pt, blk=blk: e.tensor_copy(out=V1[:, blk, :], in_=pt[:, 0:256]),
                               w=[pb, B_V1])
                        if own:
                            op("act", lambda e, pt=pt, c0=c0, cw=cw, vs=vs: e.copy(out=vs[:, c0:c0 + cw], in_=pt[:, 0:cw]),
                               w=[pb, B_vs])
                            if c0 == 512:
                                pt_buf[0] = pb
                                v3_store(pt, SEG + T * 128)
                    if own:
                        op("sp", lambda e, vs=vs, T=T: e.dma_start(
                            out=kv_own[T * 128:(T + 1) * 128, 1, :, :], in_=vs[:, :].rearrange("p (h e) -> p h e", e=64)),
                           w=[B_vs], dma=B_vs)
                for bk in backs:
                    bk()
            if own or u == 11:
                blk = (1 + s_) if own else 0
                for r_ in range(4):
                    pt, pb = bank()
                    for kc in range(8):
                        mm(pt[:, 0:256], xT[:, kc, r_:512:4], wv_[:, kc, 256:512], kc == 0, kc == 7,
                           r=[B_xT, B_wv], w=[pb])
                    evac(V2[:, r_, blk, :], pt[:, 0:256], r=[], w=[pb, B_V2])
            if not own:
                for t in range(4):
                    pt, pb = bank()
                    for kc in range(8):
                        mm(pt[:, 0:256], xT[:, kc, t * 128:(t + 1) * 128], wv_[:, kc, 512:768], kc == 0, kc == 7,
                           r=[B_xT, B_wv], w=[pb])
                    pt_buf[0] = pb
                    v3_store(pt, hs * 512 + t * 128)

    for blk in range(2):
        op("sp", lambda e, blk=blk: e.dma_start(
            out=V3[:, :, blk, :], in_=v3s[blk * SEG:(blk + 1) * SEG, :].rearrange("(i r) c -> i r c", r=16)),
           r=[B_v3s], w=[B_V3], dma=B_V3)

    PHASES.append(('C', {e_: len(S.q[e_]) for e_ in ENGS}))
    S.barrier()
    AM.off = mark_B
    release_copies(28, B_c)
    copy_every[0] = 2
    acc, B_acc = AM.alloc([128, 2, 2, SEG], F32)
    PT2 = [AM.alloc([128, 4, 256], BF16) for _ in range(2)]
    PM2 = [AM.alloc([128, 4, 256], BF16) for _ in range(2)]
    KT = (K1T, K2T, K3T); QT = (Q1T, Q2T, Q3T); VT = (V1, V2, V3)
    B_K = (B_K1, B_K2, B_K3); B_Q = (B_Q1, B_Q2, B_Q3); B_V = (B_V1, B_V2, B_V3)
    it = 0
    import os
    for g in [int(c_) for c_ in os.environ.get('KCG', '012')]:
        d = DIL[g]
        nblk = SEG // (128 * d)
        for r_ in range(d):
            for blk in range(1, nblk + 1):
                PT, B_PT = PT2[it % 2]
                PM, B_PM = PM2[it % 2]
                it += 1
                psS = [bank(), bank()]
                for h in range(4):
                    j_ = h // 2
                    p0 = 64 * (h % 2)
                    pt, pb = psS[h % 2]
                    for kbi, kb in enumerate((blk - 1, blk)):
                        if g == 0:
                            kv_ = K1T[p0:p0 + 64, j_, kb * 128:(kb + 1) * 128]
                            qv_ = Q1T[p0:p0 + 64, j_, (blk - 1) * 128:blk * 128]
                        else:
                            kv_ = KT[g][p0:p0 + 64, j_, r_, kb * 128:(kb + 1) * 128]
                            qv_ = QT[g][p0:p0 + 64, j_, r_, (blk - 1) * 128:blk * 128]
                        c0 = (h // 2) * 256 + kbi * 128
                        mm(pt[:, c0:c0 + 128], kv_, qv_, True, True, r=[B_K[g], B_Q[g]], w=[pb])
                for hp in range(2):
                    pt, pb = psS[hp]
                    op("act", lambda e, pt=pt, hp=hp, PT=PT: e.activation(
                        out=PT[:, hp:4:2, :], in_=pt[:, :].rearrange("p (h k) -> p h k", k=256),
                        func=AF.Exp, scale=0.125), w=[pb, B_PT])
                op("dve", lambda e, PT=PT, PM=PM: e.tensor_tensor(
                    out=PM, in0=PT, in1=maskP.unsqueeze(1).broadcast_to([128, 4, 256]), op=ALU.mult),
                   r=[B_PT, B_c], w=[B_PM])
                pN, pNb = bank()
                for h in range(4):
                    j_ = h // 2
                    p0 = 64 * (h % 2)
                    for nd in range(2):
                        for kbi, kb in enumerate((blk - 1, blk)):
                            if nd == 0:
                                lt_ = V1[:, kb, h * 64:(h + 1) * 64] if g == 0 else VT[g][:, r_, kb, h * 64:(h + 1) * 64]
                            else:
                                lt_ = hval_b if kb == 0 else ones_b
                            c0 = (nd * 2 + j_) * 128
                            mm(pN[p0:p0 + 64, c0:c0 + 128], lt_, PM[:, h, kbi * 128:(kbi + 1) * 128],
                               kbi == 0, kbi == 1, r=[B_V[g], B_PM, B_c], w=[pNb])
                tok0 = d * 128 * (blk - 1) + r_
                av = acc[:, :, :, tok0:tok0 + d * 127 + 1:d]
                pv4 = pN[:, :].rearrange("p (n j q) -> p n j q", n=2, j=2)
                if g == 0:
                    op("act", lambda e, av=av, pv4=pv4: e.copy(out=av, in_=pv4), w=[pNb, B_acc])
                else:
                    op("dve", lambda e, av=av, pv4=pv4: e.tensor_tensor(out=av, in0=av, in1=pv4, op=ALU.add),
                       w=[pNb, B_acc])
    if os.environ.get('KCF', '1') == '1':
        op("dve", lambda e: e.reciprocal(out=acc[:, 1, :, :], in_=acc[:, 1, :, :]), w=[B_acc])
        op("dve", lambda e: e.tensor_tensor(out=oAT, in0=acc[:, 0, :, :], in1=acc[:, 1, :, :], op=ALU.mult),
           r=[B_acc], w=[B_oAT])

    if os.environ.get("KSTOP") == "C":
        op("sp", lambda e: e.dma_start(out=st_own.rearrange("h d v -> d h v"), in_=S_f), w=[B_Sf], dma=B_Sf)
        S.barrier()
        S.op("sp", None)
        S.emit()
        return
    PHASES.append(('D', {e_: len(S.q[e_]) for e_ in ENGS}))
    S.barrier()
    AM.reset()
    vbc, B_vbc = AM.alloc([128, 5, 1024], F32)
    for i_ in range(5):
        op("sp", lambda e, i_=i_: e.dma_start(out=vbc[:, i_, :], in_=vecs[i_:i_ + 1, :].broadcast_to([128, D])),
           w=[B_vbc], dma=B_vbc)
    gng_bc = vbc[:, 0, :]
    xb, B_xb = AM.alloc([128, 4, 1024], BF16)
    xT, B_xT = AM.alloc([128, 8, 512], BF16)
    mark_gla = AM.off
    alloc_gla(True)
    end_gla = AM.off
    xres, B_xr = AM.alloc([128, 4, 1024], F32)
    rsil, B_rs = AM.alloc([128, 4, 1024], BF16)
    ob, B_ob = xb, B_xb
    obT, B_obT = AM.alloc([128, 8, 512], BF16)
    mgT, B_mg = AM.alloc([128, 8, 512], BF16)
    tmpA, B_tA = AM.alloc([128, 1024], F32)
    tmpB, B_tB = AM.alloc([128, 512], F32)
    ppb, B_ppb = AM.alloc([128, 4, 256], BF16)
    ppT, B_ppT = AM.alloc([128, 2, 512], BF16)
    st6, _ = AM.alloc([128, 4, 2, 6], F32)
    mv, _ = AM.alloc([128, 4, 4], F32)
    B_stt = [Buf('st%d' % i_) for i_ in range(4)]
    B_mvt = [Buf('mv%d' % i_) for i_ in range(4)]
    ss, B_ss = AM.alloc([128, 8], F32)
    save = AM.off
    AM.off = mark_gla
    hT, B_hT = AM.alloc([128, 22, 512], BF16)
    assert AM.off <= end_gla
    AM.off = save
    GLA_BUFS = list({id(v[1]): v[1] for k_, v in G.items() if not k_.startswith('x')}.values())

    def transposes_from_xb(NT, which=None):
        for kc in range(8):
            pt, pb = bank()
            pv = pt[:, :].bitcast(BF16)
            for t in range(NT):
                op("pe", lambda e, t=t, kc=kc, pv=pv: e.transpose(pv[:, t * 128:(t + 1) * 128],
                                                                   xb[:, t, kc * 128:(kc + 1) * 128], ident_b),
                   r=[B_xb, B_c], w=[pb])
            if which is None:
                evac(xT[:, kc, 0:NT * 128], pv[:, 0:NT * 128], r=[], w=[pb, B_xT])
            elif kc % 2 == 0:
                op("act", lambda e, kc=kc, pv=pv: e.activation(
                    out=xT[:, kc, 0:NT * 128], in_=pv[:, 0:NT * 128], func=AF.Identity,
                    scale=lncol[:, 2 * which, kc:kc + 1], bias=lncol[:, 2 * which + 1, kc:kc + 1]), r=[B_c], w=[pb, B_xT])
            else:
                op("dve", lambda e, kc=kc, pv=pv: e.tensor_scalar(
                    out=xT[:, kc, 0:NT * 128], in0=pv[:, 0:NT * 128], scalar1=lncol[:, 2 * which, kc:kc + 1],
                    scalar2=lncol[:, 2 * which + 1, kc:kc + 1], op0=ALU.mult, op1=ALU.add), r=[B_c], w=[pb, B_xT])

    def layer_norm(which, NT):
        g_bc = vbc[:, 1 + 2 * which, :]
        b_bc = vbc[:, 2 + 2 * which, :]
        for t in range(NT):
            B_st, B_mv = B_stt[t], B_mvt[t]
            for hf in range(2):
                op("dve", lambda e, t=t, hf=hf: e.bn_stats(out=st6[:, t, hf, :], in_=xres[:, t, hf * 512:(hf + 1) * 512]),
                   r=[B_xr], w=[B_st])
            op("dve", lambda e, t=t: e.bn_aggr(out=mv[:, t, 0:2], in_=st6[:, t, :, :].rearrange("p a b -> p (a b)")),
               r=[B_st], w=[B_mv])
            op("act", lambda e, t=t: e.activation(out=mv[:, t, 2:3], in_=mv[:, t, 1:2], func=AF.Sqrt, bias=eps_ln),
               r=[B_c], w=[B_mv])
            op("dve", lambda e, t=t: e.reciprocal(out=mv[:, t, 2:3], in_=mv[:, t, 2:3]), w=[B_mv])
            op("dve", lambda e, t=t: e.scalar_tensor_tensor(out=mv[:, t, 3:4], in0=mv[:, t, 0:1], scalar=-1.0,
                                                            in1=mv[:, t, 2:3], op0=ALU.mult, op1=ALU.mult), w=[B_mv])
            op("act", lambda e, t=t: e.activation(out=xb[:, t, :], in_=xres[:, t, :], func=AF.Identity,
                                                  scale=mv[:, t, 2:3], bias=mv[:, t, 3:4]), r=[B_mv, B_xr], w=[B_xb])
        for t in range(NT):
            B_mv = B_mvt[t]
            op("pool", lambda e, t=t: e.tensor_scalar(out=xres[:, t, :], in0=xres[:, t, :], scalar1=mv[:, t, 2:3],
                                                      scalar2=mv[:, t, 3:4], op0=ALU.mult, op1=ALU.add),
               r=[B_mv], w=[B_xr])
            op("pool", lambda e, t=t: e.tensor_tensor(out=xres[:, t, :], in0=xres[:, t, :], in1=g_bc, op=ALU.mult),
               r=[B_vbc], w=[B_xr])
            op("pool", lambda e, t=t: e.tensor_tensor(out=xres[:, t, :], in0=xres[:, t, :], in1=b_bc, op=ALU.add),
               r=[B_vbc], w=[B_xr])

    def o_cb(t, po):
        for hp in range(2):
            pt, pb = po[hp]
            op("act", lambda e, pt=pt, hp=hp: e.activation(out=tmpA[:, hp * 512:(hp + 1) * 512], in_=pt[:, :],
                                                           func=AF.Square), w=[pb, B_tA])
        op("dve", lambda e: e.tensor_reduce(out=ss[:, 0:4], in_=tmpA[:, :].rearrange("p (h v) -> p h v", v=256),
                                            axis=AX.X, op=ALU.add), r=[B_tA], w=[B_ss])
        op("act", lambda e: e.activation(out=ss[:, 4:8], in_=ss[:, 0:4], func=AF.Sqrt, scale=1.0 / 256.0,
                                         bias=eps_rms), r=[B_c], w=[B_ss])
        op("dve", lambda e: e.reciprocal(out=ss[:, 4:8], in_=ss[:, 4:8]), w=[B_ss])
        for h in range(4):
            pt, pb = po[h // 2]
            c0 = (h % 2) * 256
            op("dve", lambda e, pt=pt, c0=c0, h=h, t=t: e.scalar_tensor_tensor(
                out=ob[:, t, h * 256:(h + 1) * 256], in0=pt[:, c0:c0 + 256], scalar=ss[:, 4 + h:5 + h],
                in1=rsil[:, t, h * 256:(h + 1) * 256], op0=ALU.mult, op1=ALU.mult), r=[B_ss, B_rs], w=[pb, B_ob])

    def dense(NT, gla_fn, oA_fn, y_store):
        NK = NT * 128
        for hf in range(2):
            wr, B_wr = wload(w_in, 0, 8, C_RG + hf * 512, 512)
            for t in range(NT):
                pt, pb = bank()
                for kc in range(8):
                    mm(pt[:, :], xT[:, kc, t * 128:(t + 1) * 128], wr[:, kc, :], kc == 0, kc == 7, r=[B_xT, B_wr], w=[pb])
                op("act", lambda e, pt=pt: e.activation(out=tmpB, in_=pt[:, :], func=AF.Silu), w=[pb, B_tB])
                op("dve", lambda e, t=t, hf=hf: e.tensor_tensor(
                    out=rsil[:, t, hf * 512:(hf + 1) * 512], in0=tmpB, in1=gng_bc[:, hf * 512:(hf + 1) * 512], op=ALU.mult),
                   r=[B_tB, B_vbc], w=[B_rs])
        gla_fn()
        for t in range(NT):
            pt, pb = bank()
            pv = pt[:, :].bitcast(BF16)
            for kc in range(8):
                op("pe", lambda e, t=t, kc=kc, pv=pv: e.transpose(pv[:, kc * 128:(kc + 1) * 128],
                                                                   ob[:, t, kc * 128:(kc + 1) * 128], ident_b),
                   r=[B_ob, B_c], w=[pb])
            evac(obT[:, :, t * 128:(t + 1) * 128], pv[:, :].rearrange("p (k q) -> p k q", q=128), r=[], w=[pb, B_obT])
        for pas in range(2):
            for cg in range(2):
                if pas == 0:
                    wp_, B_wp = wload(w_a_out, 0, 2, cg * 512, 512)
                    wg_, B_wg_ = wload(w_in, 0, 8, C_GA + cg * 512, 512)
                    nk_ = 2
                else:
                    wp_, B_wp = wload(w_b_out, 0, 8, cg * 512, 512)
                    wg_, B_wg_ = wload(w_in, 0, 8, C_GB + cg * 512, 512)
                    nk_ = 8
                for c in range(4):
                    ch = cg * 4 + c
                    pg, pgb = bank()
                    for kc in range(8):
                        mm(pg[:, 0:NK], wg_[:, kc, c * 128:(c + 1) * 128], xT[:, kc, 0:NK], kc == 0, kc == 7,
                           r=[B_wg_, B_xT], w=[pgb])
                    op("act", lambda e, pg=pg: e.activation(out=tmpB[:, 0:NK], in_=pg[:, 0:NK], func=AF.Sigmoid),
                       w=[pgb, B_tB])
                    pp_, ppb_ = bank()
                    for kc in range(nk_):
                        if pas == 0:
                            rhs_, rb = oA_fn(kc)
                        else:
                            rhs_ = obT[:, kc, 0:NK]
                            rb = B_obT
                        mm(pp_[:, 0:NK], wp_[:, kc, c * 128:(c + 1) * 128], rhs_, kc == 0, kc == nk_ - 1, r=[B_wp, rb], w=[ppb_])
                    if pas == 0:
                        op("dve", lambda e, pp_=pp_, ch=ch: e.tensor_tensor(out=mgT[:, ch, 0:NK], in0=tmpB[:, 0:NK],
                                                                          in1=pp_[:, 0:NK], op=ALU.mult),
                           r=[B_tB], w=[ppb_, B_mg])
                    else:
                        op("dve", lambda e, pp_=pp_: e.tensor_tensor(out=tmpB[:, 0:NK], in0=tmpB[:, 0:NK], in1=pp_[:, 0:NK],
                                                                    op=ALU.mult), w=[ppb_, B_tB])
                        op("dve", lambda e, ch=ch: e.tensor_tensor(out=mgT[:, ch, 0:NK], in0=mgT[:, ch, 0:NK],
                                                                  in1=tmpB[:, 0:NK], op=ALU.add), r=[B_tB], w=[B_mg])
        for hf in range(2):
            wo_, B_wo = wload(w_o, 0, 8, hf * 512, 512)
            for t in range(NT):
                pt, pb = bank()
                for kc in range(8):
                    mm(pt[:, :], mgT[:, kc, t * 128:(t + 1) * 128], wo_[:, kc, :], kc == 0, kc == 7, r=[B_mg, B_wo], w=[pb])
                op("dve", lambda e, pt=pt, t=t, hf=hf: e.scalar_tensor_tensor(
                    out=xres[:, t, hf * 512:(hf + 1) * 512], in0=xres[:, t, hf * 512:(hf + 1) * 512], scalar=ALPHA,
                    in1=pt[:, :], op0=ALU.mult, op1=ALU.add), w=[pb, B_xr])
        layer_norm(0, NT)
        transposes_from_xb(NT, 0)
        if NT == 1:
            S.barrier()
        fc = 0
        for c0 in range(0, DFF, 256):
            wgu_, B_wgu = wload2(w_fg, w_fu, 0, 8, c0, 256)
            for c in range(2):
                pg, pgb = bank()
                pu, pub = bank()
                for kc in range(8):
                    mm(pg[:, 0:NK], wgu_[:, kc, c * 128:(c + 1) * 128], xT[:, kc, 0:NK], kc == 0, kc == 7, r=[B_wgu, B_xT], w=[pgb])
                for kc in range(8):
                    mm(pu[:, 0:NK], wgu_[:, kc, 256 + c * 128:256 + (c + 1) * 128], xT[:, kc, 0:NK], kc == 0, kc == 7,
                       r=[B_wgu, B_xT], w=[pub])
                op("act", lambda e, pg=pg: e.activation(out=tmpB[:, 0:NK], in_=pg[:, 0:NK], func=AF.Silu), w=[pgb, B_tB])
                op("dve", lambda e, pu=pu, fc=fc: e.tensor_tensor(out=hT[:, fc, 0:NK], in0=tmpB[:, 0:NK], in1=pu[:, 0:NK],
                                                                  op=ALU.mult), r=[B_tB], w=[pub, B_hT] + (GLA_BUFS if fc == 0 else []))
                fc += 1
        for hf in range(2):
            pbs = [bank() for _ in range(NT)]
            for f0, nf in ((0, 8), (8, 8), (16, 6)):
                wd_, B_wd = wload(w_fd, f0, nf, hf * 512, 512)
                for t in range(NT):
                    pt, pb = pbs[t]
                    for fi in range(nf):
                        mm(pt[:, :], hT[:, f0 + fi, t * 128:(t + 1) * 128], wd_[:, fi, :], f0 + fi == 0, f0 + fi == 21,
                           r=[B_hT, B_wd] + (GLA_BUFS if fi == nf - 1 else []), w=[pb])
            for t in range(NT):
                pt, pb = pbs[t]
                op("dve", lambda e, pt=pt, t=t, hf=hf: e.scalar_tensor_tensor(
                    out=xres[:, t, hf * 512:(hf + 1) * 512], in0=xres[:, t, hf * 512:(hf + 1) * 512], scalar=ALPHA,
                    in1=pt[:, :], op0=ALU.mult, op1=ALU.add), w=[pb, B_xr])
        layer_norm(1, NT)
        transposes_from_xb(NT, 1)
        for kc in range(2):
            pt, pb = bank()
            pv = pt[:, :].bitcast(BF16)
            for t in range(NT):
                op("pe", lambda e, t=t, kc=kc, pv=pv: e.transpose(pv[:, t * 128:(t + 1) * 128],
                                                                   ppb[:, t, kc * 128:(kc + 1) * 128], ident_b),
                   r=[B_ppb, B_c], w=[pb])
            evac(ppT[:, kc, 0:NK], pv[:, 0:NK], r=[], w=[pb, B_ppT])
        for hf in range(2):
            wg_, B_wg_ = wload(w_pg, 0, 8, hf * 512, 512)
            wp_, B_wp = wload(w_pp, 0, 2, hf * 512, 512)
            for t in range(NT):
                pg, pgb = bank()
                pp_, ppb_ = bank()
                for kc in range(8):
                    mm(pg[:, :], xT[:, kc, t * 128:(t + 1) * 128], wg_[:, kc, :], kc == 0, kc == 7, r=[B_xT, B_wg_], w=[pgb])
                for kc in range(2):
                    mm(pp_[:, :], ppT[:, kc, t * 128:(t + 1) * 128], wp_[:, kc, :], kc == 0, kc == 1, r=[B_ppT, B_wp], w=[ppb_])
                op("act", lambda e, pg=pg: e.activation(out=tmpB, in_=pg[:, :], func=AF.Sigmoid), w=[pgb, B_tB])
                op("dve", lambda e, pp_=pp_: e.tensor_tensor(out=tmpB, in0=tmpB, in1=pp_[:, :], op=ALU.mult),
                   w=[ppb_, B_tB])
                op("dve", lambda e, t=t, hf=hf: e.tensor_tensor(
                    out=xres[:, t, hf * 512:(hf + 1) * 512], in0=xres[:, t, hf * 512:(hf + 1) * 512], in1=tmpB, op=ALU.add),
                   r=[B_tB], w=[B_xr])
        y_store()

    for s_ in range(4):
        tok0 = NPRE + 512 * s_
        load_xT(tok0, xT, B_xT, xb, B_xb)
        op("sp", lambda e, tok0=tok0: e.dma_start(
            out=xres, in_=xseg[tok0:tok0 + 512, :].rearrange("(t p) d -> p t d", p=128)), w=[B_xr], dma=B_xr)
        op("pool", lambda e, s_=s_: e.dma_start(
            out=ppb, in_=pp[512 * s_:512 * s_ + 512, :].rearrange("(t p) d -> p t d", p=128)), w=[B_ppb], dma=B_ppb)
        dense(4, lambda: gla_super(xT, B_xT, True, o_cb),
              lambda kc, s_=s_: (oAT[:, kc, 512 * s_:512 * s_ + 512], B_oAT),
              lambda s_=s_: op("sp", lambda e: e.dma_start(
                  out=y_own[512 * s_:512 * s_ + 512, :].rearrange("(t p) d -> p t d", p=128), in_=xres),
                  w=[B_xr], dma=B_xr))

    op("sp", lambda e: e.dma_start(out=st_own.rearrange("h d v -> d h v"), in_=S_f), w=[B_Sf], dma=B_Sf)
    S.barrier()
    PHASES.append(('E', {e_: len(S.q[e_]) for e_ in ENGS}))
    NB[0] = 6
    psi[0] = 0
    save_off = AM.off
    AM.off = mark_gla
    NDEEP = 3
    hs, B_hs = AM.alloc([128, 4368], F32)
    rts, B_rts = AM.alloc([128, 4, 12, 8], F32)
    prod, B_prod = tmpA[:, 0:768], B_tA
    pnew, B_pn = AM.alloc([128, 24], F32)
    qTs, B_qTs = AM.alloc([128, 6, 16], F32)
    Qbd, B_Qbd = AM.alloc([128, 6, 16, 4], BF16)
    KV2 = [AM.alloc([128, 512], F32) for _ in range(NDEEP)]
    Vb2 = [AM.alloc([128, 512], BF16) for _ in range(NDEEP)]
    KTs2 = [AM.alloc([128, 256], BF16) for _ in range(NDEEP)]
    PTs2 = [AM.alloc([128, 4], BF16) for _ in range(NDEEP)]
    nds, B_nds = AM.alloc([128, 192], F32)
    ndtok = xres[:, 1:3, :].rearrange("p a b -> p (a b)")[:, 0:1536].rearrange("p (g c) -> p g c", c=512)
    B_ndt = Buf("ndtok")
    numt, B_numt = AM.alloc([128, 256], F32)
    dent, B_dent = AM.alloc([128, 256], F32)
    t1s, B_t1s = tmpB[:, 0:256], B_tB
    oasb, B_oasb = AM.alloc([128, 256], BF16)
    oATs, B_oATs = AM.alloc([128, 2, 128], BF16)
    glrTs, B_glrTs = AM.alloc([16, 128], BF16)
    Ls, B_Ls = AM.alloc([128, 512], F32)
    fm, B_fm = AM.alloc([128, 3, 4, 16], F32)
    ksel, B_ksel = AM.alloc([128, 16, 128], F32)
    qsel, B_qsel = AM.alloc([128, 4, 16, 16], F32)
    S02 = [AM.alloc([128, 256], F32) for _ in range(2)]
    Sn2 = [AM.alloc([128, 256], F32) for _ in range(2)]
    o_s, B_os = AM.alloc([128, 1024], F32)
    assert AM.off <= end_gla, (AM.off, end_gla)
    AM.off = save_off
    b6, B_b6 = psb[6]
    b7, B_b7 = psb[7]

    op("pool", lambda e: e.dma_start(out=xb[:, 0, :], in_=xs), w=[B_xb], dma=B_xb)
    op("sp", lambda e: e.dma_start(out=xres[:, 0, :], in_=xs), w=[B_xr], dma=B_xr)
    op("pool", lambda e: e.dma_start(out=ppb[:, 0, :], in_=pps), w=[B_ppb], dma=B_ppb)
    transposes_from_xb(1)
    op("dve", lambda e: e.memset(o_s, 0.0), w=[B_os])
    op("dve", lambda e: e.memset(oasb, 0.0), w=[B_oasb])
    for (c0, ncol, d0) in ((0, 768, 0), (768, 768, 768), (1536, 768, 1536), (2304, 512, 2304), (2816, 512, 2816),
                           (3328, 512, 3328), (3840, 512, 3840), (C_GLR, 16, 4352)):
        wv_, B_wv = wload(w_in, 0, 8, c0, ncol)
        for sub in range(0, ncol, 512):
            cw = min(512, ncol - sub)
            pt, pb = bank()
            for kc in range(8):
                mm(pt[:, 0:cw], xT[:, kc, 0:128], wv_[:, kc, sub:sub + cw], kc == 0, kc == 7, r=[B_xT, B_wv], w=[pb])
            evac(hs[:, d0 + sub:d0 + sub + cw], pt[:, 0:cw], r=[], w=[pb, B_hs])
    for qk in range(2):
        qv = hs[:, qk * 768:(qk + 1) * 768].rearrange("p (h e) -> p h e", e=64)
        x1 = qv[:, :, 0:8]
        x2 = qv[:, :, 8:16]
        cs = rope_s[:, 0:8].unsqueeze(1).broadcast_to([128, 12, 8])
        sn = rope_s[:, 8:16].unsqueeze(1).broadcast_to([128, 12, 8])
        for k_, (a, b_) in enumerate(((x1, cs), (x2, sn), (x2, cs), (x1, sn))):
            op("dve", lambda e, k_=k_, a=a, b_=b_: e.tensor_tensor(out=rts[:, k_, :, :], in0=a, in1=b_, op=ALU.mult),
               r=[B_hs, B_c], w=[B_rts])
        op("dve", lambda e, x1=x1: e.tensor_tensor(out=x1, in0=rts[:, 0, :, :], in1=rts[:, 1, :, :], op=ALU.subtract),
           r=[B_rts], w=[B_hs])
        op("dve", lambda e, x2=x2: e.tensor_tensor(out=x2, in0=rts[:, 2, :, :], in1=rts[:, 3, :, :], op=ALU.add),
           r=[B_rts], w=[B_hs])
    kvs_out = (kvs1, kvs2, kvs3)
    caches = (c1, c2, c3)
    WIN = (128, 512, 2048)
    for g in range(3):
        op("sp", lambda e, g=g: e.dma_start(out=kvs_out[g][:, WIN[g] - 1, 0:256], in_=hs[0:16, 768 + 256 * g:1024 + 256 * g]),
           w=[B_hs], dma=B_hs)
        op("sp", lambda e, g=g: e.dma_start(out=kvs_out[g][:, WIN[g] - 1, 256:512], in_=hs[0:16, 1536 + 256 * g:1792 + 256 * g]),
           w=[B_hs], dma=B_hs)
    op("dve", lambda e: e.tensor_tensor(out=prod, in0=hs[:, 0:768], in1=hs[:, 768:1536], op=ALU.mult), r=[B_hs], w=[B_prod])
    op("dve", lambda e: e.tensor_reduce(out=pnew[:, 0:12], in_=prod[:, :].rearrange("p (h e) -> p h e", e=64),
                                        axis=AX.X, op=ALU.add), r=[B_prod], w=[B_pn])
    op("act", lambda e: e.activation(out=pnew[:, 12:24], in_=pnew[:, 0:12], func=AF.Exp, scale=0.125), w=[B_pn])
    for c2_ in range(2):
        pt, pb = bank()
        for cc in range(3):
            c = c2_ * 3 + cc
            op("pe", lambda e, c=c, cc=cc, pt=pt: e.transpose(pt[:, cc * 128:(cc + 1) * 128], hs[:, c * 128:(c + 1) * 128], ident_f),
               r=[B_hs, B_c], w=[pb])
        evac(qTs[:, c2_ * 3:c2_ * 3 + 3, :], pt[:, 0:384].rearrange("p (c q) -> p c q", q=128)[:, :, 0:16], r=[], w=[pb, B_qTs])
    for c in range(6):
        op("dve", lambda e, c=c: e.tensor_tensor(
            out=Qbd[:, c, :, :], in0=qTs[:, c, :].unsqueeze(2).broadcast_to([128, 16, 4]),
            in1=bmask[:, c % 2, :].unsqueeze(1).broadcast_to([128, 16, 4]), op=ALU.mult), r=[B_qTs, B_c], w=[B_Qbd])
    it = 0
    for g in range(3):
        d = DIL[g]
        for n in range(NS):
            KV, B_KV = KV2[it % NDEEP]
            Vb, B_Vb = Vb2[it % NDEEP]
            KTs, B_KTs = KTs2[it % NDEEP]
            PTs, B_PTs = PTs2[it % NDEEP]
            it += 1
            op("sp", lambda e, g=g, n=n, d=d, KV=KV: e.dma_start(out=KV, in_=caches[g][n, 0:WIN[g]:d, :]), w=[B_KV], dma=B_KV)
            op("pool", lambda e, KV=KV, Vb=Vb: e.tensor_copy(out=Vb, in_=KV), r=[B_KV], w=[B_Vb])
            pt, pb = bank()
            pvb = pt[:, :].bitcast(BF16)
            for j_ in range(2):
                op("pe", lambda e, j_=j_, pvb=pvb, Vb=Vb: e.transpose(pvb[:, j_ * 128:(j_ + 1) * 128], Vb[:, j_ * 128:(j_ + 1) * 128], ident_b),
                   r=[B_Vb, B_c], w=[pb])
            evac(KTs, pvb[:, 0:256], r=[], w=[pb, B_KTs])
            ps_, psb_ = bank()
            for j_ in range(2):
                mm(ps_[:, 0:4], KTs[:, j_ * 128:(j_ + 1) * 128], Qbd[:, 2 * g + j_, n, :], j_ == 0, j_ == 1,
                   r=[B_KTs, B_Qbd], w=[psb_])
            op("act", lambda e, ps_=ps_, PTs=PTs: e.activation(out=PTs, in_=ps_[:, 0:4], func=AF.Exp, scale=0.125),
               w=[psb_, B_PTs])
            for h in range(4):
                p0 = 64 * (h % 2)
                for nd in range(2):
                    col = ((g * 2 + nd) * 2 + h // 2) * 16 + n
                    lt_ = Vb[:, 256 + h * 64:256 + (h + 1) * 64] if nd == 0 else ones_b
                    mm(b7[p0:p0 + 64, col:col + 1], lt_, PTs[:, h:h + 1], True, True, r=[B_Vb, B_PTs, B_c], w=[B_b7])
    op("act", lambda e: e.copy(out=nds, in_=b7[:, 0:192]), w=[B_b7, B_nds])
    for g in range(3):
        pt, pb = bank()
        for q_ in range(4):
            col = (g * 4 + q_) * 16
            op("pe", lambda e, q_=q_, col=col, pt=pt: e.transpose(pt[0:16, q_ * 128:(q_ + 1) * 128], nds[:, col:col + 16], ident_f),
               r=[B_nds, B_c], w=[pb])
        evac(ndtok[0:16, g, :], pt[0:16, :], r=[], w=[pb, B_ndt])
    R = slice(0, 16)
    op("dve", lambda e: e.tensor_tensor(out=numt[R, :], in0=ndtok[R, 0, 0:256], in1=ndtok[R, 1, 0:256], op=ALU.add), r=[B_ndt], w=[B_numt])
    op("dve", lambda e: e.tensor_tensor(out=numt[R, :], in0=numt[R, :], in1=ndtok[R, 2, 0:256], op=ALU.add), r=[B_ndt], w=[B_numt])
    op("dve", lambda e: e.tensor_tensor(out=dent[R, :], in0=ndtok[R, 0, 256:512], in1=ndtok[R, 1, 256:512], op=ALU.add), r=[B_ndt], w=[B_dent])
    op("dve", lambda e: e.tensor_tensor(out=dent[R, :], in0=dent[R, :], in1=ndtok[R, 2, 256:512], op=ALU.add), r=[B_ndt], w=[B_dent])
    for g in range(3):
        pb_ = pnew[R, 12 + 4 * g:16 + 4 * g].unsqueeze(2).broadcast_to([16, 4, 64])
        vn_ = hs[R, 1536 + 256 * g:1792 + 256 * g].rearrange("p (h e) -> p h e", e=64)
        op("dve", lambda e, pb_=pb_, vn_=vn_: e.tensor_tensor(out=t1s[R, :].rearrange("p (h e) -> p h e", e=64), in0=vn_, in1=pb_, op=ALU.mult),
           r=[B_pn, B_hs], w=[B_t1s])
        op("dve", lambda e: e.tensor_tensor(out=numt[R, :], in0=numt[R, :], in1=t1s[R, :], op=ALU.add), r=[B_t1s], w=[B_numt])
        op("dve", lambda e, pb_=pb_: e.tensor_tensor(out=dent[R, :].rearrange("p (h e) -> p h e", e=64),
                                                    in0=dent[R, :].rearrange("p (h e) -> p h e", e=64), in1=pb_, op=ALU.add),
           r=[B_pn], w=[B_dent])
    op("dve", lambda e: e.reciprocal(out=dent[R, :], in_=dent[R, :]), w=[B_dent])
    op("dve", lambda e: e.tensor_tensor(out=oasb[R, :], in0=numt[R, :], in1=dent[R, :], op=ALU.mult), r=[B_numt, B_dent], w=[B_oasb])
    pt, pb = bank()
    pv = pt[:, :].bitcast(BF16)
    for j_ in range(2):
        op("pe", lambda e, j_=j_, pv=pv: e.transpose(pv[:, j_ * 128:(j_ + 1) * 128], oasb[:, j_ * 128:(j_ + 1) * 128], ident_b),
           r=[B_oasb, B_c], w=[pb])
    evac(oATs, pv[:, 0:256].rearrange("p (j q) -> p j q", q=128), r=[], w=[pb, B_oATs])

    def sample_gla():
        pt, pb = bank()
        op("pe", lambda e, pt=pt: e.transpose(pt[0:16, 0:128], hs[:, 4352:4368], ident_f), r=[B_hs, B_c], w=[pb])
        evac(glrTs, pt[0:16, 0:128], r=[], w=[pb, B_glrTs])
        pt, pb = bank()
        mm(pt[:, :], glrTs, wgu_b, True, True, r=[B_glrTs, B_c], w=[pb])
        op("dve", lambda e, pt=pt: e.tensor_tensor(out=Ls, in0=pt[:, :], in1=bgate_bc, op=ALU.add), r=[B_c], w=[pb, B_Ls])
        op("act", lambda e: e.activation(out=Ls, in_=Ls, func=AF.Exp, scale=-1.0), w=[B_Ls])
        op("act", lambda e: e.activation(out=Ls, in_=Ls, func=AF.Ln, bias=one_c), r=[B_c], w=[B_Ls])
        op("act", lambda e: e.activation(out=Ls, in_=Ls, func=AF.Exp, scale=-1.0 / 16.0), w=[B_Ls])
        for wi_, src in enumerate((Ls, hs[:, 2816:3328], hs[:, 2304:2816])):
            pt, pb = bank()
            for h in range(4):
                op("pe", lambda e, h=h, pt=pt, src=src: e.transpose(pt[:, h * 128:(h + 1) * 128], src[:, h * 128:(h + 1) * 128], ident_f),
                   r=[B_Ls, B_hs, B_c], w=[pb])
            evac(fm[:, wi_, :, :], pt[:, :].rearrange("p (h q) -> p h q", q=128)[:, :, 0:16], r=[], w=[pb, B_fm])
        for h in range(4):
            op("dve", lambda e, h=h: e.scalar_tensor_tensor(
                out=qsel[:, h, :, :], in0=fm[:, 2, h, :].unsqueeze(2).broadcast_to([128, 16, 16]), scalar=float(128 ** -0.5),
                in1=eye16b[:, :].rearrange("p (a b) -> p a b", b=16), op0=ALU.mult, op1=ALU.mult), r=[B_fm, B_c], w=[B_qsel])
        i_ = 0
        for h in range(4):
            op("dve", lambda e, h=h: e.tensor_tensor(
                out=ksel[R, :, :], in0=hs[R, 2816 + h * 128:2944 + h * 128].unsqueeze(1).broadcast_to([16, 16, 128]),
                in1=ident_f[R, 0:16].unsqueeze(2).broadcast_to([16, 16, 128]), op=ALU.mult), r=[B_hs, B_c], w=[B_ksel])
            for n in range(NS):
                S0, B_S0 = S02[i_ % 2]
                Sn, B_Sn = Sn2[i_ % 2]
                i_ += 1
                op("sp", lambda e, n=n, h=h, S0=S0: e.dma_start(out=S0, in_=st_in[n, h, :, :]), w=[B_S0], dma=B_S0)
                ps_, psb_ = bank()
                mm(ps_[:, 0:256], ksel[R, n, :], hs[R, 3328 + h * 256:3584 + h * 256], True, True, r=[B_ksel, B_hs], w=[psb_])
                op("dve", lambda e, ps_=ps_, S0=S0, Sn=Sn, h=h, n=n: e.scalar_tensor_tensor(
                    out=Sn, in0=S0, scalar=fm[:, 0, h, n:n + 1], in1=ps_[:, 0:256], op0=ALU.mult, op1=ALU.add),
                   r=[B_S0, B_fm], w=[psb_, B_Sn])
                mm(b6[0:16, 0:256], qsel[:, h, n, :], Sn, n == 0, n == NS - 1, r=[B_qsel, B_Sn], w=[B_b6])
                op("sp", lambda e, n=n, h=h, Sn=Sn: e.dma_start(out=st_out[n, h, :, :], in_=Sn), w=[B_Sn], dma=B_Sn)
            op("act", lambda e, h=h: e.copy(out=o_s[R, h * 256:(h + 1) * 256], in_=b6[0:16, 0:256]), w=[B_b6, B_os])
        op("act", lambda e: e.activation(out=tmpA, in_=o_s, func=AF.Square), r=[B_os], w=[B_tA])
        op("dve", lambda e: e.tensor_reduce(out=ss[:, 0:4], in_=tmpA[:, :].rearrange("p (h v) -> p h v", v=256),
                                            axis=AX.X, op=ALU.add), r=[B_tA], w=[B_ss])
        op("act", lambda e: e.activation(out=ss[:, 4:8], in_=ss[:, 0:4], func=AF.Sqrt, scale=1.0 / 256.0,
                                         bias=eps_rms), r=[B_c], w=[B_ss])
        op("dve", lambda e: e.reciprocal(out=ss[:, 4:8], in_=ss[:, 4:8]), w=[B_ss])
        for h in range(4):
            op("dve", lambda e, h=h: e.scalar_tensor_tensor(
                out=ob[:, 0, h * 256:(h + 1) * 256], in0=o_s[:, h * 256:(h + 1) * 256], scalar=ss[:, 4 + h:5 + h],
                in1=rsil[:, 0, h * 256:(h + 1) * 256], op0=ALU.mult, op1=ALU.mult), r=[B_ss, B_rs, B_os], w=[B_ob])

    dense(1, sample_gla, lambda kc: (oATs[:, kc, :], B_oATs),
          lambda: op("sp", lambda e: e.dma_start(out=y_s, in_=xres[:, 0, :]), w=[B_xr], dma=B_xr))
    release_copies(999, B_c)
    S.barrier()
    S.op("sp", None)
    S.emit()


def _rope_tables():
    inv = (500000.0 ** (-np.arange(0, 16, 2, dtype=np.float32) / np.float32(16))).astype(np.float32)
    return inv


_PROG = None
TAGS = None
PHASES = []
_RAW = [False]


def kernel(x_prompt, x_sample, cache_a1_kv, cache_a2_kv, cache_a3_kv, state_gla, p_prompt, p_sample,
           w_in, w_gate_up, b_gate, gla_norm_g, w_a_out, w_b_out, w_o, ln1_g, ln1_b,
           w_ff_gate, w_ff_up, w_ff_down, ln2_g, ln2_b, w_ple_gate, w_ple_proj):
    global _PROG
    f = lambda a: np.ascontiguousarray(np.asarray(a, dtype=np.float32))
    x_prompt = f(x_prompt)
    if _PROG is None:
        _PROG = build_program()
    nc = _PROG
    ii = np.arange(128)
    cst = np.zeros((128, 640), np.float32)
    cst[:, 0:128] = np.eye(128, dtype=np.float32)
    cst[:, 128:256] = (ii[:, None] <= ii[None, :])
    cst[:, 256:384] = (ii[:, None] >= ii[None, :])
    cst[:, 384:448] = 1.0
    vecs = np.zeros((6, D), np.float32)
    vecs[0] = f(gla_norm_g)[0]; vecs[1] = f(ln1_g)[0]; vecs[2] = f(ln1_b)[0]
    vecs[3] = f(ln2_g)[0]; vecs[4] = f(ln2_b)[0]; vecs[5, 0:512] = f(b_gate)[0]
    inv = _rope_tables()
    shared = dict(w_in=f(w_in)[0], w_gu=f(w_gate_up)[0], vecs=vecs, w_a_out=f(w_a_out)[0], w_b_out=f(w_b_out)[0],
                  w_o=f(w_o)[0], w_fg=f(w_ff_gate)[0], w_fu=f(w_ff_up)[0], w_fd=f(w_ff_down)[0],
                  w_pg=f(w_ple_gate)[0], w_pp=f(w_ple_proj)[0])
    x_sample = f(x_sample); p_sample = f(p_sample); state_gla = f(state_gla)
    cache_a1_kv = f(cache_a1_kv); cache_a2_kv = f(cache_a2_kv); cache_a3_kv = f(cache_a3_kv)
    ang_s = np.float32(SEQ) * inv
    rope_s = np.tile(np.concatenate([np.cos(ang_s), np.sin(ang_s)]).astype(np.float32)[None, :], (128, 1))
    cst2 = np.zeros((128, 296), np.float32)
    for i_ in range(4):
        cst2[:, 264 + 8 * i_:272 + 8 * i_] = vecs[1 + i_].reshape(8, 128).T
    cst2[:, 0:256] = np.eye(16, dtype=np.float32).reshape(1, 256)
    for p_ in range(128):
        for j_ in range(2):
            cst2[p_, 256 + j_ * 4 + 2 * j_ + p_ // 64] = 1.0
    in_maps = []
    for c in range(NCORES):
        b, j = divmod(c, 4)
        t0 = j * SEG
        xs = np.zeros((NPRE + SEG, D), np.float32)
        lo = t0 - NPRE
        src_lo = max(lo, 0)
        xs[src_lo - lo:] = x_prompt[b, src_lo:t0 + SEG]
        pos = np.maximum(np.arange(t0 - SEG, t0 + SEG), 0).astype(np.float32)
        ang = pos[:, None] * inv[None, :]
        tab = np.concatenate([np.cos(ang), np.sin(ang)], axis=1).astype(np.float32)
        tab = np.ascontiguousarray(tab.reshape(32, 128, 16).transpose(1, 0, 2))
        cc = cst.copy()
        cc[:, 512:576] = 1.0 if j > 0 else 0.0
        m = dict(shared)
        m.update(xseg=xs, pp=f(p_prompt)[0, b, t0:t0 + SEG], rope=tab, cst=cc)
        n0 = NS * c
        xsp = np.zeros((128, D), np.float32)
        xsp[:NS] = x_sample[n0:n0 + NS, 0]
        ppp = np.zeros((128, 256), np.float32)
        ppp[:NS] = p_sample[0, n0:n0 + NS, 0]
        m.update(xs=xsp, pps=ppp, rope_s=rope_s, cst2=cst2,
                 c1=cache_a1_kv[0, n0:n0 + NS].reshape(NS, 128, 512),
                 c2=cache_a2_kv[0, n0:n0 + NS].reshape(NS, 512, 512),
                 c3=cache_a3_kv[0, n0:n0 + NS].reshape(NS, 2048, 512),
                 st_in=state_gla[0, n0:n0 + NS])
        in_maps.append(m)
    res = run_bass_kernel_spmd(nc, in_maps, core_ids=list(range(NCORES))).results
    if _RAW[0]:
        return res
    y_prompt = np.stack([np.concatenate([res[4 * b + j]["y_own"] for j in range(4)], axis=0) for b in range(2)])
    kvp = []
    for g, keep in enumerate((128, 512, 2048)):
        kvp.append(np.stack([res[4 * b + 3]["kv_own"][SEG - keep:, :, 4 * g:4 * g + 4, :] for b in range(2)])[None])
    st_p = np.stack([res[4 * b + 3]["st_own"] for b in range(2)])[None]
    y_sample = np.concatenate([res[c]["y_s"][:NS] for c in range(NCORES)], axis=0)[:, None, :]
    kvs = []
    for g, (nm, w_) in enumerate((("kvs1", 128), ("kvs2", 512), ("kvs3", 2048))):
        kvs.append(np.concatenate([res[c][nm] for c in range(NCORES)], axis=0).reshape(1, 128, w_, 2, 4, 64))
    st_s = np.concatenate([res[c]["st_out"] for c in range(NCORES)], axis=0)[None]
    return (y_prompt.astype(np.float32), y_sample, kvp[0], kvp[1], kvp[2], st_p,
            kvs[0], kvs[1], kvs[2], st_s)
```
